# Optimizing a Trainium2 kernel written in Bass

```python
import jax, jax.numpy as jnp
from jax import lax
import numpy as np

D_MODEL = 2048
BATCH = 16
SEQ = 256
DEPTH = 1
DEC_BATCH = 4
DEC_SEQ = 4096
PAST_LEN = 512

GRID_W = 64
HEAD_DIM = 128
N_Q_HEADS = 8
N_KV_HEADS = 2
Q_PER_KV = N_Q_HEADS // N_KV_HEADS
ATTN_WIDTH = N_Q_HEADS * HEAD_DIM
KV_WIDTH = N_KV_HEADS * HEAD_DIM
POOL_WINDOWS = (2, 4, 8, 16)
N_POOL_GROUPS = len(POOL_WINDOWS)
POOL_WIDTH = D_MODEL - ATTN_WIDTH
POOL_GROUP_WIDTH = POOL_WIDTH // N_POOL_GROUPS
MIX_WIDTH = ATTN_WIDTH + POOL_WIDTH
IN_WIDTH = ATTN_WIDTH + 2 * KV_WIDTH + POOL_WIDTH
D_FF = 4 * D_MODEL
Q_BLOCK = 128
ROPE_THETA = 10000.0
ROPE_QUARTER = HEAD_DIM // 4
EPS = 1e-6
N_MOD = 6
DEEPNORM_ALPHA = (2.0 * DEPTH) ** 0.25
DEEPNORM_BETA = (8.0 * DEPTH) ** -0.25

kernel_name = "hybrid_pool_gqa_prefix_diffusion_step"


def _layer_norm(x):
    xf = x.astype(jnp.float32)
    mu = jnp.mean(xf, axis=-1, keepdims=True)
    var = jnp.mean(jnp.square(xf - mu), axis=-1, keepdims=True)
    return (xf - mu) * lax.rsqrt(var + EPS)


def _rms_norm(x, gain):
    xf = x.astype(jnp.float32)
    return xf * lax.rsqrt(jnp.mean(jnp.square(xf), axis=-1, keepdims=True) + EPS) * gain.astype(jnp.float32)


def _rotate(x, cos, sin):
    x1, x2 = jnp.split(x, 2, axis=-1)
    return jnp.concatenate([x1 * cos - x2 * sin, x2 * cos + x1 * sin], axis=-1)


def _apply_rope(x, rope):
    cos_r, sin_r, cos_c, sin_c = rope
    xr, xc = jnp.split(x.astype(jnp.float32), 2, axis=-1)
    return jnp.concatenate([_rotate(xr, cos_r, sin_r), _rotate(xc, cos_c, sin_c)], axis=-1)


def _grid_rope(n_tokens):
    n_rows = n_tokens // GRID_W
    row = jnp.repeat(jnp.arange(n_rows, dtype=jnp.float32), GRID_W)
    col = jnp.tile(jnp.arange(GRID_W, dtype=jnp.float32), n_rows)
    inv_freq = ROPE_THETA ** (-(jnp.arange(ROPE_QUARTER, dtype=jnp.float32) / ROPE_QUARTER))
    ang_r = (row[:, None] * inv_freq)[:, None, :]
    ang_c = (col[:, None] * inv_freq)[:, None, :]
    return (jnp.cos(ang_r), jnp.sin(ang_r), jnp.cos(ang_c), jnp.sin(ang_c))


def _block_attention(q, k, v):
    b, t = q.shape[0], q.shape[1]
    nb = t // Q_BLOCK
    qb = q.reshape(b, nb, Q_BLOCK, N_KV_HEADS, Q_PER_KV, HEAD_DIM).transpose(1, 0, 2, 3, 4, 5)
    kf = k.astype(jnp.float32)
    vf = v.astype(jnp.float32)
    scale = HEAD_DIM ** -0.5

    def one_block(q_blk):
        s = jnp.einsum('bqkgd,bskd->bkgqs', q_blk.astype(jnp.float32), kf) * scale
        p = jax.nn.softmax(s, axis=-1)
        return jnp.einsum('bkgqs,bskd->bqkgd', p, vf)

    out = lax.map(one_block, qb)
    return out.transpose(1, 0, 2, 3, 4, 5).reshape(b, t, ATTN_WIDTH)


def _pool_mixer(p, w_pool, pool_scale):
    b, t, _ = p.shape
    pg = p.reshape(b, t, N_POOL_GROUPS, POOL_GROUP_WIDTH).astype(jnp.float32)
    csum = jnp.concatenate([jnp.zeros((b, 1, N_POOL_GROUPS, POOL_GROUP_WIDTH), jnp.float32),
                            jnp.cumsum(pg, axis=1)], axis=1)
    pos = jnp.arange(t)
    outs = []
    for g, w in enumerate(POOL_WINDOWS):
        lo = jnp.clip(pos - w // 2, 0, t)
        hi = jnp.clip(pos + w // 2, 0, t)
        cg = csum[:, :, g]
        s = jnp.take(cg, hi, axis=1) - jnp.take(cg, lo, axis=1)
        cnt = (hi - lo).astype(jnp.float32)[None, :, None]
        outs.append(s / cnt - pg[:, :, g])
    pooled = jnp.stack(outs, axis=2)
    mixed = jnp.einsum('btgc,gce->btge', pooled, w_pool.astype(jnp.float32))
    return (mixed.reshape(b, t, POOL_WIDTH) * pool_scale.astype(jnp.float32)).astype(p.dtype)


def _layer(x, mod, w_in, q_gain, k_gain, w_pool, pool_scale, w_out,
           ln1_g, ln1_b, w_ff1, w_ff2, ln2_g, ln2_b, rope=None, k_ctx=None, v_ctx=None):
    dt = x.dtype
    b, t, _ = x.shape
    shift1, scale1, gate1 = mod[:, None, 0], mod[:, None, 1], mod[:, None, 2]
    shift2, scale2, gate2 = mod[:, None, 3], mod[:, None, 4], mod[:, None, 5]

    u = (_layer_norm(x) * (1.0 + scale1) + shift1).astype(dt)
    proj = jnp.einsum('btd,de->bte', u, w_in)
    q_raw, k_raw, v_raw, p = jnp.split(proj, [ATTN_WIDTH, ATTN_WIDTH + KV_WIDTH, ATTN_WIDTH + 2 * KV_WIDTH], axis=-1)
    q = _rms_norm(q_raw.reshape(b, t, N_Q_HEADS, HEAD_DIM), q_gain)
    k = _rms_norm(k_raw.reshape(b, t, N_KV_HEADS, HEAD_DIM), k_gain)
    v = v_raw.reshape(b, t, N_KV_HEADS, HEAD_DIM)
    if rope is not None:
        q = _apply_rope(q, rope)
        k = _apply_rope(k, rope)
    q = q.astype(dt)
    k = k.astype(dt)
    if k_ctx is not None:
        keys = jnp.concatenate([k_ctx.astype(dt), k], axis=1)
        vals = jnp.concatenate([v_ctx.astype(dt), v], axis=1)
    else:
        keys, vals = k, v
    attn = _block_attention(q, keys, vals).astype(dt)
    pool = _pool_mixer(p, w_pool, pool_scale)
    mix = jnp.einsum('bte,ed->btd', jnp.concatenate([attn, pool], axis=-1), w_out)
    x1 = (_layer_norm(DEEPNORM_ALPHA * x + gate1 * mix) * ln1_g + ln1_b).astype(dt)

    u2 = (_layer_norm(x1) * (1.0 + scale2) + shift2).astype(dt)
    h = jnp.square(jax.nn.relu(jnp.einsum('btd,df->btf', u2, w_ff1)))
    f = jnp.einsum('btf,fd->btd', h, w_ff2)
    x2 = (_layer_norm(DEEPNORM_ALPHA * x1 + gate2 * f) * ln2_g + ln2_b).astype(dt)
    return x2, k, v


def setup_inputs(seed: int = 0) -> dict:
    key = jax.random.key(seed)
    ks = jax.random.split(key, 24)
    f32 = jnp.float32
    nrm = lambda k, s: jax.random.normal(k, s, f32)
    return {
        "x_prompt": nrm(ks[0], (BATCH, SEQ, D_MODEL)),
        "x_sample": nrm(ks[1], (DEC_BATCH, DEC_SEQ, D_MODEL)),
        "cache_k": nrm(ks[2], (DEC_BATCH, DEPTH, PAST_LEN, N_KV_HEADS, HEAD_DIM)),
        "cache_v": nrm(ks[3], (DEC_BATCH, DEPTH, PAST_LEN, N_KV_HEADS, HEAD_DIM)),
        "c": nrm(ks[4], (DEC_BATCH, D_MODEL)),
        "c_ctx": nrm(ks[5], (D_MODEL,)),
        "w_mod": nrm(ks[6], (DEPTH, D_MODEL, N_MOD * D_MODEL)) * (0.5 * D_MODEL ** -0.5),
        "b_mod": nrm(ks[7], (DEPTH, N_MOD * D_MODEL)) * 0.02,
        "w_in": nrm(ks[8], (DEPTH, D_MODEL, IN_WIDTH)) * D_MODEL ** -0.5,
        "q_gain": 1.0 + 0.1 * nrm(ks[9], (DEPTH, HEAD_DIM)),
        "k_gain": 1.0 + 0.1 * nrm(ks[10], (DEPTH, HEAD_DIM)),
        "w_pool": nrm(ks[11], (DEPTH, N_POOL_GROUPS, POOL_GROUP_WIDTH, POOL_GROUP_WIDTH)) * POOL_GROUP_WIDTH ** -0.5,
        "pool_scale": 1.0 + 0.1 * nrm(ks[12], (DEPTH, POOL_WIDTH)),
        "w_out": nrm(ks[13], (DEPTH, MIX_WIDTH, D_MODEL)) * (MIX_WIDTH ** -0.5 * DEEPNORM_BETA),
        "ln1_g": 1.0 + 0.1 * nrm(ks[14], (DEPTH, D_MODEL)),
        "ln1_b": 0.02 * nrm(ks[15], (DEPTH, D_MODEL)),
        "w_ff1": nrm(ks[16], (DEPTH, D_MODEL, D_FF)) * D_MODEL ** -0.5,
        "w_ff2": nrm(ks[17], (DEPTH, D_FF, D_MODEL)) * (D_FF ** -0.5 * DEEPNORM_BETA),
        "ln2_g": 1.0 + 0.1 * nrm(ks[18], (DEPTH, D_MODEL)),
        "ln2_b": 0.02 * nrm(ks[19], (DEPTH, D_MODEL)),
    }


def reference(x_prompt, x_sample, cache_k, cache_v, c, c_ctx, w_mod, b_mod, w_in, q_gain, k_gain,
              w_pool, pool_scale, w_out, ln1_g, ln1_b, w_ff1, w_ff2, ln2_g, ln2_b):
    rope = _grid_rope(x_sample.shape[1])
    h_ctx = x_prompt
    h_lat = x_sample
    new_k, new_v = [], []
    for l in range(DEPTH):
        mod_ctx = (jax.nn.silu(c_ctx) @ w_mod[l] + b_mod[l]).reshape(1, N_MOD, D_MODEL)
        mod_lat = (jax.nn.silu(c) @ w_mod[l] + b_mod[l]).reshape(c.shape[0], N_MOD, D_MODEL)
        lw = (w_in[l], q_gain[l], k_gain[l], w_pool[l], pool_scale[l], w_out[l],
              ln1_g[l], ln1_b[l], w_ff1[l], w_ff2[l], ln2_g[l], ln2_b[l])
        h_ctx, k_c, v_c = _layer(h_ctx, mod_ctx, *lw)
        new_k.append(k_c)
        new_v.append(v_c)
        h_lat, _, _ = _layer(h_lat, mod_lat, *lw, rope=rope,
                             k_ctx=cache_k[:, l], v_ctx=cache_v[:, l])
    ctx_k = jnp.stack(new_k, axis=1)
    ctx_v = jnp.stack(new_v, axis=1)
    return (h_ctx, h_lat, ctx_k, ctx_v)
```

```python
import numpy as np
import concourse.bass as bass
import concourse.mybir as mybir
from concourse.bass_utils import run_bass_kernel_spmd

F32 = mybir.dt.float32
BF16 = mybir.dt.bfloat16
AF = mybir.ActivationFunctionType
ALU = mybir.AluOpType

D = 2048
EPS = 1e-6
ALPHA = float(2.0 ** 0.25)
ATT_SCALE = float(128 ** -0.5)
EXP_SHIFT = -10.0
N_CORES = 8
SAME_ENGINE_WAITS = True
DEBUG = False


class Res:
    __slots__ = ("name", "w", "r")

    def __init__(self, name):
        self.name = name
        self.w = {}
        self.r = {}

    def set_w(self, tok):
        self.w = {tok[0]: (tok[1], False)}
        self.r = {}


class DSem:
    def __init__(self, sem):
        self.sem = sem
        self.cnt = 0


class Eng:
    def __init__(self, name, sem):
        self.name = name
        self.sem = sem
        self.cnt = 0
        self.ops = []
        self.waited = {}


class Prog:
    def __init__(self, engs):
        self.E = engs
        self.dsems = []

    def _deps(self, E, reads, writes, swrites):
        need = {}

        def add(s, v):
            if need.get(s, 0) < v:
                need[s] = v

        for r in reads:
            for s, (v, _) in r.w.items():
                add(s, v)
            if r.name.startswith("bank"):
                for s, v in r.r.items():
                    if s != E.sem:
                        add(s, v)
        for w in writes:
            for s, (v, _) in w.w.items():
                add(s, v)
            for s, v in w.r.items():
                add(s, v)
        for w in swrites:
            for s, (v, sh) in w.w.items():
                if not sh:
                    add(s, v)
            for s, v in w.r.items():
                add(s, v)
        for s, v in need.items():
            if s == E.sem:
                if E.name == "pe" or not SAME_ENGINE_WAITS:
                    continue
                v = min(v, E.cnt)
                if v <= 0:
                    continue
            if E.waited.get(s, 0) < v:
                E.waited[s] = v
                E.ops.append(("wait", s, v))

    def _post(self, tok, reads, writes, swrites):
        s, v = tok
        for r in reads:
            if r.r.get(s, 0) < v:
                r.r[s] = v
        for w in writes:
            w.w = {s: (v, False)}
            w.r = {}
        for w in swrites:
            if w.r:
                w.w = {s: (v, True)}
                w.r = {}
            else:
                old = w.w.get(s)
                if old is None or old[1]:
                    w.w[s] = (v, True)
                else:
                    w.w[s] = (v, False)

    def op(self, eng, fn, reads=(), writes=(), swrites=(), inc=True):
        E = self.E[eng]
        self._deps(E, reads, writes, swrites)
        if inc:
            E.cnt += 1
            tok = (E.sem, E.cnt)
        else:
            tok = (E.sem, E.cnt + 1)
        E.ops.append(("op", fn, inc))
        self._post(tok, reads, writes, swrites)

    def dma(self, q, out, in_, dsem, reads=(), writes=()):
        E = self.E[q]
        self._deps(E, reads, writes, ())
        dsem.cnt += 16
        tok = (dsem.sem, dsem.cnt)
        E.ops.append(("dma", out, in_, dsem.sem))
        self._post(tok, reads, writes, ())

    def barrier(self):
        for E in self.E.values():
            for O in self.E.values():
                if O is E or O.cnt == 0:
                    continue
                if E.waited.get(O.sem, 0) < O.cnt:
                    E.waited[O.sem] = O.cnt
                    E.ops.append(("wait", O.sem, O.cnt))
            for d in self.dsems:
                if d.cnt and E.waited.get(d.sem, 0) < d.cnt:
                    E.waited[d.sem] = d.cnt
                    E.ops.append(("wait", d.sem, d.cnt))


def emit(e, E):
    for o in E.ops:
        if o[0] == "wait":
            e.wait_ge(o[1], o[2])
        elif o[0] == "op":
            ins = o[1](e)
            if o[2]:
                ins.then_inc(E.sem, 1)
        else:
            e.dma_start(out=o[1], in_=o[2]).then_inc(o[3], 16)


def pipeline(items, stages):
    n, S = len(items), len(stages)
    for step in range(n + S - 1):
        for s in reversed(range(S)):
            t = step - s
            if 0 <= t < n:
                stages[s](items[t])


class Carver:
    def __init__(self, big, base=0):
        self.big = big
        self.off = base

    def f32(self, n):
        ap = self.big[:, self.off:self.off + n]
        self.off += n
        return ap

    def bf16(self, n):
        words = (n + 1) // 2
        ap = self.big[:, self.off:self.off + words].bitcast(BF16)
        self.off += words
        return ap


def build_program():
    nc = bass.Bass("TRN2", target_bir_lowering=False)

    def din(name, shape):
        return nc.dram_tensor(name, list(shape), F32, kind="ExternalInput").ap()

    def dout(name, shape):
        return nc.dram_tensor(name, list(shape), F32, kind="ExternalOutput").ap()

    xs = din("xs", [4096, D])
    xp = din("xp", [512, D])
    ck = din("ck", [512, 256])
    cv = din("cv", [512, 256])
    cvec = din("cvec", [32, 128])
    w_mod = din("w_mod", [D, 6 * D])
    b_mod = din("b_mod", [96, 128])
    w_in = din("w_in", [D, 2560])
    qg = din("qg", [1, 128])
    kg = din("kg", [1, 128])
    w_pool = din("w_pool", [1024, 256])
    pool_scale = din("pool_scale", [8, 128])
    w_out = din("w_out", [D, D])
    ln1_g = din("ln1_g", [1, D])
    ln1_b = din("ln1_b", [1, D])
    ln2_g = din("ln2_g", [1, D])
    ln2_b = din("ln2_b", [1, D])
    w_ff1 = din("w_ff1", [D, 4 * D])
    w_ff2 = din("w_ff2", [4 * D, D])
    rope = din("rope", [32 * 128, 256])
    bands = din("bands", [128, 36 * 128])
    ident = din("ident", [128, 128])

    ys = dout("ys", [2048, D])
    yp = dout("yp", [512, D])
    cko = dout("cko", [512, 256])
    cvo = dout("cvo", [512, 256])
    x1s = nc.dram_tensor("x1s", [2560, D], F32, kind="Internal").ap()
    N_SLAB_A, N_SLAB_B = 9, 32
    wsc = nc.dram_tensor("wsc", [(N_SLAB_A + N_SLAB_B) * 128, 8192], BF16, kind="Internal").ap()

    uTs = nc.dram_tensor("uTs", [18 * 128, 2048], BF16, kind="Internal").ap()
    NW = 53200
    from contextlib import ExitStack
    with ExitStack() as _stk:
        big_t = _stk.enter_context(nc.sbuf_tensor("big", [128, NW], F32))
        ps_t = _stk.enter_context(nc.psum_tensor("ps", [128, 4096], F32))
        s_pe, s_act, s_dve, s_pool, s_sync = [_stk.enter_context(nc.semaphore(n)) for n in
                                              ("s_pe", "s_act", "s_dve", "s_pool", "s_sync")]
        _dl = [_stk.enter_context(nc.semaphore("d%d" % i)) for i in range(38)]
        big = big_t[:, :]
        psa = ps_t[:, :]
        engs = {
            "pe": Eng("pe", s_pe), "act": Eng("act", s_act), "dve": Eng("dve", s_dve),
            "pool": Eng("pool", s_pool), "sync": Eng("sync", s_sync),
        }
        P = Prog(engs)
        dpool = [DSem(s) for s in _dl]
        P.dsems = dpool
        dnext = [0]

        def new_dsem():
            d = dpool[dnext[0]]
            dnext[0] += 1
            return d

        PSB = [psa[:, b * 512:(b + 1) * 512] for b in range(8)]
        PSBb = [psa[:, b * 512:(b + 1) * 512].bitcast(BF16) for b in range(8)]
        PSR = [Res("bank%d" % b) for b in range(8)]

        def ps2(b0):
            return psa[:, b0 * 512:(b0 + 2) * 512]

        cvr = Carver(big)
        ident_f = cvr.f32(128)
        ident_b = cvr.bf16(128)
        ones_b = cvr.bf16(128)
        modT = cvr.f32(192).rearrange("p (m s) -> p m s", s=2)
        scT = cvr.bf16(32)
        bmT = cvr.f32(96)
        psT = cvr.f32(8)
        eps_t = cvr.f32(8)
        _stg = Carver(big, 60000)
        mvs = [cvr.f32(8) for _ in range(4)]
        sts = [cvr.f32(24) for _ in range(4)]
        ssb = [cvr.f32(8) for _ in range(4)]
        tab_g = cvr.f32(D)
        tab_b = cvr.f32(D)
        slab_bufs = [cvr.bf16(16 * 512) for _ in range(2)]
        xhats = [cvr.bf16(D) for _ in range(2)]
        pers_end = cvr.off
        _stg = Carver(big, pers_end + 30000)
        cv_sb = _stg.f32(128)
        bm_sb = _stg.f32(128)
        ps_sb = _stg.f32(128)

        R = {}

        def res(name):
            if name not in R:
                R[name] = Res(name)
            return R[name]

        slab_res = [res("slab0"), res("slab1"), res("slab2")]
        slab_ds = [new_dsem(), new_dsem(), new_dsem()]
        slab_ds_hw = [new_dsem(), new_dsem(), new_dsem()]
        slab_state = {"n": 2, "i": 0}
        mv_i = [0]

        def slab_view(i):
            return slab_bufs[i].rearrange("p (k n) -> p k n", n=512)

        def slab_load(src3d, buf=None):
            if buf is None:
                i = slab_state["i"] % slab_state["n"]
                slab_state["i"] += 1
            else:
                i = buf
            v = slab_view(i)
            P.dma("pool", v, src3d, slab_ds[i], writes=[slab_res[i]])
            return v, slab_res[i]

        def wslab(w, c0, k0=0):
            return w.rearrange("(k p) n -> p k n", p=128)[:, k0:k0 + 16, c0:c0 + 512]

        def scr(idx):
            return wsc[idx * 128:(idx + 1) * 128, :].rearrange("p (k n) -> p k n", n=512)

        conv_d = [[new_dsem(), new_dsem()], [new_dsem(), new_dsem(), new_dsem(), new_dsem()]]
        conv_n = [0, 0]
        r_conv = [res("convA"), res("convB")]

        def convert(batch, idx, src3d):
            E = engs["pool"]
            ds = conv_d[batch]
            d = ds[conv_n[batch] % len(ds)]
            conv_n[batch] += 1
            if d.cnt > 0 and E.waited.get(d.sem, 0) < d.cnt:
                E.waited[d.sem] = d.cnt
                E.ops.append(("wait", d.sem, d.cnt))
            P.dma("pool", scr(idx), src3d, d)

        def conv_done(batch):
            r_conv[batch].w = {d.sem: (d.cnt, False) for d in conv_d[batch] if d.cnt}

        def slab_from_scratch(batch, idx):
            i = slab_state["i"] % slab_state["n"]
            slab_state["i"] += 1
            v = slab_view(i)
            P.dma("sync", v, scr(idx), slab_ds_hw[i], reads=[r_conv[batch]], writes=[slab_res[i]])
            return v, slab_res[i]

        misc_d = new_dsem()

        P.dma("sync", ident_f, ident, misc_d, writes=[res("ident_f")])
        P.dma("sync", cv_sb[0:32, :], cvec, misc_d, writes=[res("cv_sb")])
        P.dma("sync", bm_sb[0:96, :], b_mod, misc_d, writes=[res("bm_sb")])
        P.dma("sync", ps_sb[0:8, :], pool_scale, misc_d, writes=[res("ps_sb")])
        for _n in ("ident_f", "cv_sb", "bm_sb", "ps_sb"):
            res(_n).w = {misc_d.sem: (misc_d.cnt, False)}
        P.op("dve", lambda e: e.tensor_copy(out=ident_b, in_=ident_f),
             reads=[res("ident_f")], writes=[res("ident_b")])
        P.op("dve", lambda e: e.memset(ones_b, 1.0), writes=[res("ones_b")])
        P.op("dve", lambda e: e.memset(eps_t, EPS), writes=[res("eps_t")])
        P.op("dve", lambda e: e.memset(eps_t[:, 1:2], EXP_SHIFT), writes=[res("eps_t")])
        P.op("act", lambda e: e.activation(out=cv_sb[0:32, :], in_=cv_sb[0:32, :], func=AF.Silu),
             reads=[res("cv_sb")], writes=[res("cv_sb")])
        P.op("pe", lambda e: e.transpose(out=PSB[0][:, 0:32], in_=cv_sb[0:32, :], identity=ident_f[0:32, 0:32]),
             reads=[res("cv_sb"), res("ident_f")], writes=[PSR[0]])
        P.op("dve", lambda e: e.tensor_copy(out=scT, in_=PSB[0][:, 0:32]), reads=[PSR[0]], writes=[res("scT")])
        P.op("pe", lambda e: e.transpose(out=PSB[1][:, 0:96], in_=bm_sb[0:96, :], identity=ident_f[0:96, 0:96]),
             reads=[res("bm_sb"), res("ident_f")], writes=[PSR[1]])
        P.op("dve", lambda e: e.tensor_copy(out=bmT, in_=PSB[1][:, 0:96]), reads=[PSR[1]], writes=[res("bmT")])
        P.op("pe", lambda e: e.transpose(out=PSB[2][:, 0:8], in_=ps_sb[0:8, :], identity=ident_f[0:8, 0:8]),
             reads=[res("ps_sb"), res("ident_f")], writes=[PSR[2]])
        P.op("dve", lambda e: e.tensor_copy(out=psT, in_=PSB[2][:, 0:8]), reads=[PSR[2]], writes=[res("psT")])

        scT3 = scT.rearrange("p (s k) -> p s k", k=16)

        def mod_slabs(s0, s1, buf=None, fixed_bank=None, preloaded=None):
            for s in range(s0, s1):
                if preloaded is None:
                    sv, sr = slab_load(wslab(w_mod, s * 512), buf=buf)
                else:
                    sv, sr = preloaded
                bank = (6 + (s % 2)) if fixed_bank is None else fixed_bank
                for j in range(4):
                    for kc in range(16):
                        P.op("pe", lambda e, j=j, kc=kc, sv=sv, bank=bank: e.matmul(
                            PSB[bank][:, 2 * j:2 * j + 2], lhsT=sv[:, kc, j * 128:(j + 1) * 128],
                            rhs=scT3[:, :, kc], start=(kc == 0), stop=(kc == 15)),
                            reads=[sr, res("scT")], writes=[PSR[bank]], inc=(kc == 15 and j == 3))
                P.op("dve", lambda e, s=s, bank=bank: e.tensor_tensor(
                    out=modT[:, 4 * s:4 * s + 4, :],
                    in0=PSB[bank][:, 0:8].rearrange("p (j s) -> p j s", s=2),
                    in1=bmT[:, 4 * s:4 * s + 4].unsqueeze(2).to_broadcast([128, 4, 2]), op=ALU.add),
                    reads=[PSR[bank], res("bmT")], swrites=[res("modTa" if s < 8 else ("modTb" if s < 12 else "modTc"))])

        def mod_plus1(m0):
            rr = res("modTa" if m0 < 32 else "modTc")
            P.op("dve", lambda e: e.tensor_scalar_add(out=modT[:, m0:m0 + 16, :], in0=modT[:, m0:m0 + 16, :], scalar1=1.0),
                 reads=[rr], writes=[rr])

        mod_slabs(0, 8)
        mod_plus1(16)

        c1 = Carver(big, pers_end)
        KT = c1.bf16(2 * 4608).rearrange("p (h n) -> p h n", n=4608)
        Vb = c1.bf16(36 * 256).rearrange("p (c n) -> p c n", n=256)
        qg_t = c1.f32(128)
        kg_t = c1.f32(128)
        bands_sb = c1.bf16(36 * 128).rearrange("p (m n) -> p m n", n=128)
        wp = c1.bf16(8 * 256).rearrange("p (a e) -> p a e", e=256)
        rope_b = [c1.f32(256) for _ in range(2)]
        _xres_off = c1.off
        xres1 = [c1.f32(D) for _ in range(2)]
        kvslab = big[:, _xres_off:_xres_off + 2 * D].bitcast(BF16).rearrange("p (k n) -> p k n", n=512)
        xtmp = [c1.f32(D) for _ in range(2)]
        uT = c1.bf16(16 * 512).rearrange("p (s k n) -> p s k n", s=4, k=16)
        mixT = c1.bf16(16 * 256).rearrange("p (k n) -> p k n", n=256)
        QT = c1.bf16(8 * 256)
        _ptok_off = c1.off
        pTok = c1.bf16(4 * 1024).rearrange("p (t n) -> p t n", n=1024)
        acc2 = big[:, _ptok_off:_ptok_off + 1024]
        pooledT = c1.bf16(8 * 256).rearrange("p (a n) -> p a n", n=256)
        kn = c1.f32(512)
        t1 = c1.f32(512)
        t2 = c1.f32(512)
        qrs = [c1.bf16(512) for _ in range(2)]
        vout = c1.f32(256)
        PTe = [c1.bf16(1024) for _ in range(3)]
        acc = c1.f32(1024)
        fT1 = c1.f32(4 * 256).rearrange("p (c n) -> p c n", n=256)
        print('c1.off', c1.off)
        assert c1.off <= NW, c1.off

        xres1_r = [res("xres0"), res("xres1")]
        xres1_d = [new_dsem(), new_dsem()]
        xtmp_r = [res("xtmp0"), res("xtmp1")]
        xtmp_d = [new_dsem(), new_dsem()]
        rope_r = [res("rope0"), res("rope1")]
        rope_d = [new_dsem(), new_dsem()]
        tab_d = new_dsem()
        tab_d2 = new_dsem()
        kn_d = new_dsem()
        vout_d = new_dsem()
        misc2_d = new_dsem()
        misc3_d = new_dsem()

        P.dma("sync", qg_t, qg.to_broadcast([128, 128]), misc2_d, writes=[res("qg")])
        P.dma("sync", kg_t, kg.to_broadcast([128, 128]), misc2_d, writes=[res("kg")])
        P.dma("sync", tab_g, ln1_g.to_broadcast([128, D]), tab_d, writes=[res("tab_g")])
        P.dma("sync", tab_b, ln1_b.to_broadcast([128, D]), tab_d2, writes=[res("tab_b")])
        P.dma("pool", bands_sb, bands.rearrange("p (m n) -> p m n", n=128), misc3_d, writes=[res("bands")])
        P.dma("pool", wp, w_pool.rearrange("(a p) e -> p a e", p=128), misc3_d, writes=[res("wp")])
        for _n in ("qg", "kg"):
            res(_n).w = {misc2_d.sem: (misc2_d.cnt, False)}
        for _n in ("bands", "wp"):
            res(_n).w = {misc3_d.sem: (misc3_d.cnt, False)}

        xhat_r = [res("xhat0"), res("xhat1")]
        xh_i = [0]

        def ln_stats(x_ap, x_res):
            i = mv_i[0] % 4
            mv_i[0] += 1
            mv, st = mvs[i], sts[i]
            r = res("mv%d" % i)
            for c in range(4):
                P.op("dve", lambda e, c=c: e.bn_stats(out=st[:, c * 6:(c + 1) * 6], in_=x_ap[:, c * 512:(c + 1) * 512]),
                     reads=[x_res], swrites=[r])
            P.op("dve", lambda e: e.bn_aggr(out=mv[:, 0:2], in_=st), reads=[r], swrites=[r])
            P.op("act", lambda e: e.activation(out=mv[:, 2:3], in_=mv[:, 1:2], func=AF.Sqrt, bias=eps_t[:, 0:1], scale=1.0),
                 reads=[r, res("eps_t")], swrites=[r])
            P.op("dve", lambda e: e.reciprocal(out=mv[:, 2:3], in_=mv[:, 2:3]), reads=[r], writes=[r])
            P.op("dve", lambda e: e.tensor_scalar(out=mv[:, 3:4], in0=mv[:, 0:1], scalar1=-1.0, scalar2=mv[:, 2:3],
                                                  op0=ALU.mult, op1=ALU.mult), reads=[r], swrites=[r])
            return mv, r

        class FrontItem:
            def __init__(self, x_ap, x_res, setc, sc_base, sh_base, dst_fn, uT_r, load=None):
                self.x_ap, self.x_res, self.setc = x_ap, x_res, setc
                self.sc_base, self.sh_base = sc_base, sh_base
                self.dst_fn, self.uT_r = dst_fn, uT_r
                self.load = load

        tr_i = [0]

        def fr_load(it):
            if it.load is not None:
                it.load()
                it.load = None

        def fr_A(it):
            fr_load(it)
            it.xi = xh_i[0] % 2
            xh_i[0] += 1
            mv, r = ln_stats(it.x_ap, it.x_res)
            xh = xhats[it.xi]
            P.op("act", lambda e: e.activation(out=xh, in_=it.x_ap, func=AF.Identity, bias=mv[:, 3:4], scale=mv[:, 2:3]),
                 reads=[it.x_res, r], writes=[xhat_r[it.xi]])

        def fr_B(it):
            it.tb = 2 * (tr_i[0] % 2)
            tr_i[0] += 1
            xh = xhats[it.xi]
            for kc in range(16):
                b = it.tb + kc // 8
                P.op("pe", lambda e, kc=kc, b=b: e.transpose(
                    out=PSBb[b][:, (kc % 8) * 128:(kc % 8 + 1) * 128],
                    in_=xh[:, kc * 128:(kc + 1) * 128], identity=ident_b),
                    reads=[xhat_r[it.xi], res("ident_b")], writes=[PSR[b]], inc=(kc % 8 == 7))

        def fr_C(it):
            for kc in range(16):
                b = it.tb + kc // 8
                src = PSBb[b][:, (kc % 8) * 128:(kc % 8 + 1) * 128]
                dst = it.dst_fn(kc)
                sc = modT[:, it.sc_base + kc, it.setc:it.setc + 1]
                sh = modT[:, it.sh_base + kc, it.setc:it.setc + 1]
                if kc < 8:
                    P.op("act", lambda e, src=src, dst=dst, sc=sc, sh=sh: e.activation(
                        out=dst, in_=src, func=AF.Identity, bias=sh, scale=sc),
                        reads=[PSR[b], res("modTa" if it.sc_base < 32 else "modTc")], swrites=[it.uT_r])
                else:
                    P.op("dve", lambda e, src=src, dst=dst, sc=sc, sh=sh: e.tensor_scalar(
                        out=dst, in0=src, scalar1=sc, scalar2=sh, op0=ALU.mult, op1=ALU.add),
                        reads=[PSR[b], res("modTa" if it.sc_base < 32 else "modTc")], swrites=[it.uT_r])

        def proj_tok(slot, sv, sr, bank):
            uT_r = r_uT[slot]
            srs = [sr] if sr is not None else [xres1_r[0], xres1_r[1]]
            for kc in range(16):
                P.op("pe", lambda e, kc=kc: e.matmul(PSB[bank], lhsT=uT[:, slot, kc, :],
                                                     rhs=sv[:, kc, :], start=(kc == 0), stop=(kc == 15)),
                     reads=[uT_r] + srs, writes=[PSR[bank]], inc=(kc == 15))

        ss_i = [0]
        qr_i = [0]

        class QK:
            pass

        def qk_E(u):
            nh, bank = u.nh, u.bank
            W = nh * 128
            i = ss_i[0] % 4
            ss_i[0] += 1
            ss = ssb[i]
            rs = res("ss%d" % i)
            for h in range(nh):
                P.op("act", lambda e, h=h: e.activation(out=t1[:, h * 128:(h + 1) * 128],
                                                        in_=PSB[bank][:, h * 128:(h + 1) * 128],
                                                        func=AF.Square, accum_out=ss[:, h:h + 1]),
                     reads=[PSR[bank]], swrites=[rs, res("t1")])
            P.op("act", lambda e: e.activation(out=ss[:, 4:4 + nh], in_=ss[:, 0:nh], func=AF.Sqrt, bias=eps_t[:, 0:1],
                                               scale=1.0 / 128), reads=[rs, res("eps_t")], swrites=[rs])
            P.op("dve", lambda e: e.reciprocal(out=ss[:, 4:4 + nh], in_=ss[:, 4:4 + nh]), reads=[rs], writes=[rs])
            for h in range(nh):
                P.op("dve", lambda e, h=h: e.scalar_tensor_tensor(
                    out=kn[:, h * 128:(h + 1) * 128], in0=PSB[bank][:, h * 128:(h + 1) * 128],
                    scalar=ss[:, 4 + h:5 + h], in1=u.gain_t, op0=ALU.mult, op1=ALU.mult),
                    reads=[PSR[bank], rs, u.gain_r], swrites=[res("kn")])
            if u.kout is not None:
                u.kout()
            u.qi = qr_i[0] % 2
            qr_i[0] += 1
            qr = qrs[u.qi]
            qr_r = res("qr%d" % u.qi)
            if u.rope is not None:
                rope_t, rope_rr = u.rope()
                C = rope_t[:, 0:128]
                S = rope_t[:, 128:256]
                kn3 = kn[:, 0:W].rearrange("p (h n) -> p h n", n=128)
                t13 = t1[:, 0:W].rearrange("p (h n) -> p h n", n=128)
                P.op("dve", lambda e: e.tensor_tensor(out=t13, in0=kn3, in1=C.unsqueeze(1).to_broadcast([128, nh, 128]),
                                                       op=ALU.mult), reads=[res("kn"), rope_rr], writes=[res("t1")])
                kn5 = kn[:, 0:W].rearrange("p (h a b c) -> p h a b c", a=2, b=2, c=32)
                t25 = t2[:, 0:W].rearrange("p (h a b c) -> p h a b c", a=2, b=2, c=32)
                S4 = S.rearrange("p (a b c) -> p a b c", a=2, b=2, c=32)
                for bsel in range(2):
                    P.op("dve", lambda e, bsel=bsel: e.tensor_tensor(
                        out=t25[:, :, :, bsel, :], in0=kn5[:, :, :, 1 - bsel, :],
                        in1=S4[:, :, bsel, :].unsqueeze(1).to_broadcast([128, nh, 2, 32]), op=ALU.mult),
                        reads=[res("kn"), rope_rr], swrites=[res("t2")])
                P.op("dve", lambda e: e.tensor_tensor(out=qr[:, 0:W], in0=t1[:, 0:W], in1=t2[:, 0:W], op=ALU.add),
                     reads=[res("t1"), res("t2")], writes=[qr_r])
            else:
                P.op("dve", lambda e: e.tensor_copy(out=qr[:, 0:W], in_=kn[:, 0:W]), reads=[res("kn")], writes=[qr_r])

        def qk_F(u):
            nh = u.nh
            W = nh * 128
            qr = qrs[u.qi]
            qr_r = res("qr%d" % u.qi)
            tbank = u.tbank
            for h in range(nh):
                P.op("pe", lambda e, h=h: e.transpose(out=PSBb[tbank][:, h * 128:(h + 1) * 128],
                                                      in_=qr[:, h * 128:(h + 1) * 128], identity=ident_b),
                     reads=[qr_r, res("ident_b")], writes=[PSR[tbank]], inc=(h == nh - 1))
            P.op("act", lambda e: u.dst_fn(e, PSBb[tbank][:, 0:W]), reads=[PSR[tbank]], swrites=[u.dst_r])

        r_KT, r_V = res("KT"), res("V")
        r_uT = [res("uT%d" % i) for i in range(4)]
        r_QT, r_mixO, r_mixP = res("QT"), res("mixO"), res("mixP")
        r_pTok, r_pooled = res("pTok"), res("pooledT")

        def load_rope(ltile):
            i = ltile % 2
            P.dma("sync", rope_b[i], rope[ltile * 128:(ltile + 1) * 128, :], rope_d[i], writes=[rope_r[i]])
            return rope_b[i], rope_r[i]

        pj_i = [0]

        def lazy_slab(batch, idx):
            box = {}

            def get():
                if "v" not in box:
                    box["v"], box["r"] = slab_from_scratch(batch, idx)
                return box["v"], box["r"]
            return get

        def make_kv_unit(slot, key_off, chunk, rope_tile, slab_get, out_rows=None):
            u = QK()
            u.nh, u.gain_t, u.gain_r = 2, kg_t, res("kg")
            u.rope = (lambda: load_rope(rope_tile)) if rope_tile is not None else None
            u.tbank = 6
            u.dst_r = r_KT

            def dst_fn(e, src):
                return e.copy(out=KT[:, :, key_off:key_off + 128], in_=src.rearrange("p (h n) -> p h n", n=128))
            u.dst_fn = dst_fn

            def D_stage():
                u.bank = 4 + (pj_i[0] % 2)
                pj_i[0] += 1
                sv, sr = slab_get()
                proj_tok(slot, sv, sr, u.bank)
                P.op("act", lambda e: e.copy(out=Vb[:, chunk, :], in_=PSB[u.bank][:, 256:512]),
                     reads=[PSR[u.bank]], swrites=[r_V])
            u.D = D_stage
            u.kout = None
            if out_rows is not None:
                def kout():
                    P.dma("sync", cko[out_rows:out_rows + 128, :], kn[:, 0:256], kn_d, reads=[res("kn")])
                    P.op("act", lambda e: e.copy(out=vout, in_=PSB[u.bank][:, 256:512]), reads=[PSR[u.bank]], writes=[res("vout")])
                    P.dma("sync", cvo[out_rows:out_rows + 128, :], vout, vout_d, reads=[res("vout")])
                u.kout = kout
            return u

        kvslab_d = new_dsem()
        P.dma("pool", kvslab, wslab(w_in, 1024), kvslab_d, writes=[xres1_r[0], xres1_r[1]])
        convA = [(sq, wslab(w_in, sq * 512)) for sq in range(2)]
        convA += [(2 + sp, wslab(w_in, 1536 + sp * 512)) for sp in range(2)]
        convA += [(4, wslab(w_in, 1024))]
        convA += [(5 + ds, wslab(w_out, ds * 512)) for ds in range(4)]
        convB = [(N_SLAB_A + s, wslab(w_ff1, s * 512)) for s in range(16)]
        convB += [(N_SLAB_A + 16 + ds * 4 + kq, wslab(w_ff2, ds * 512, kq * 16)) for ds in range(4) for kq in range(4)]

        class KVItem:
            pass

        uTst_d = [new_dsem() for _ in range(4)]
        uTld_d = [new_dsem() for _ in range(4)]
        r_uTs = res("uTs")

        def uts_rows(lt):
            idx = lt if lt <= 16 else 17
            return uTs[idx * 128:(idx + 1) * 128, :].rearrange("p (k n) -> p k n", n=128)

        def store_uT(lt):
            if lt <= 16 or lt == 31:
                sl = lt % 4
                P.dma("sync", uts_rows(lt), uT[:, sl, :, :], uTst_d[sl], reads=[r_uT[sl]])

        kv_items = []
        for lt in range(32):
            it = KVItem()
            i = lt % 2
            it.fr = FrontItem(xtmp[i], xtmp_r[i], 0, 16, 0, (lambda kc, sl=lt % 4: uT[:, sl, kc, :]), r_uT[lt % 4],
                              load=(lambda lt=lt, i=i: P.dma("sync", xtmp[i], xs[lt * 128:(lt + 1) * 128, :], xtmp_d[i],
                                                            writes=[xtmp_r[i]])))
            it.u = make_kv_unit(lt % 4, lt * 128, lt, lt, (lambda: (kvslab, None)))
            it.lt = lt
            it.mod_s = (8 + lt // 2) if (lt % 2 == 1 and lt < 12) else None
            it.conv = convA[lt // 2] if (lt % 2 == 0 and lt // 2 < len(convA)) else None
            kv_items.append(it)
        pipeline(kv_items, [lambda it: (fr_A(it.fr), (convert(0, *it.conv) if it.conv is not None else None)),
                            lambda it: fr_B(it.fr), lambda it: fr_C(it.fr),
                            lambda it: (it.u.D(), store_uT(it.lt)), lambda it: qk_E(it.u), lambda it: qk_F(it.u),
                            lambda it: (mod_slabs(it.mod_s, it.mod_s + 1, fixed_bank=7) if it.mod_s is not None else None)])
        conv_done(0)
        r_uTs.w = {d.sem: (d.cnt, False) for d in uTst_d if d.cnt}
        ckv = xtmp[0].rearrange("p (t c) -> p t c", c=256)[:, 0:4, :]
        cvv = xtmp[1].rearrange("p (t c) -> p t c", c=256)[:, 0:4, :]
        P.dma("sync", ckv, ck.rearrange("(t p) c -> p t c", p=128), xtmp_d[0], writes=[xtmp_r[0]])
        P.dma("sync", cvv, cv.rearrange("(t p) c -> p t c", p=128), xtmp_d[1], writes=[xtmp_r[1]])
        P.op("dve", lambda e: e.tensor_copy(out=xhats[0][:, 0:1024], in_=xtmp[0][:, 0:1024]), reads=[xtmp_r[0]], writes=[xhat_r[0]])
        P.op("dve", lambda e: e.tensor_copy(out=Vb[:, 32:36, :], in_=cvv), reads=[xtmp_r[1]], swrites=[r_V])
        for t in range(4):
            for h in range(2):
                P.op("pe", lambda e, t=t, h=h: e.transpose(
                    out=PSBb[6][:, (t * 2 + h) * 128:(t * 2 + h + 1) * 128],
                    in_=xhats[0][:, t * 256 + h * 128:t * 256 + (h + 1) * 128], identity=ident_b),
                    reads=[xhat_r[0], res("ident_b")], writes=[PSR[6]], inc=(t == 3 and h == 1))
        P.op("act", lambda e: e.copy(out=KT[:, :, 4096:4608].rearrange("p h (t n) -> p t h n", n=128),
                                     in_=PSBb[6][:, 0:1024].rearrange("p (t h n) -> p t h n", h=2, n=128)),
             reads=[PSR[6]], swrites=[r_KT])


        BK = {"B0_mid": 0, "Bm1_std": 1, "Bp1_std": 2, "B0_fo": 3, "Bm1_fo": 4, "B0_lo": 5, "Bp1_lo": 6,
              "B0_pf": 7, "B0_pl": 8}

        def band(kind, w):
            return bands_sb[:, BK[kind] * 4 + w, :]

        def ln_affine_store(x_ap, x_res, dst, dsem):
            mv, r = ln_stats(x_ap, x_res)
            P.op("act", lambda e: e.activation(out=x_ap, in_=x_ap, func=AF.Identity, bias=mv[:, 3:4], scale=mv[:, 2:3]),
                 reads=[x_res, r], writes=[x_res])
            P.op("dve", lambda e: e.tensor_tensor(out=x_ap, in0=x_ap, in1=tab_g, op=ALU.mult),
                 reads=[x_res, res("tab_g")], writes=[x_res])
            P.op("dve", lambda e: e.tensor_tensor(out=x_ap, in0=x_ap, in1=tab_b, op=ALU.add),
                 reads=[x_res, res("tab_b")], writes=[x_res])
            P.dma("sync", dst, x_ap, dsem, reads=[x_res])

        def make_fronts(spec):
            x_dram, own_rows, halo_rows, setc = spec["x"], spec["own_rows"], spec["halo_rows"], spec["setc"]
            fitems = {}
            for j in range(2):
                fitems[1 + j] = FrontItem(
                    xres1[j], xres1_r[j], setc, 16, 0, (lambda kc, sl=1 + j: uT[:, sl, kc, :]), r_uT[1 + j],
                    load=(lambda j=j: P.dma("sync", xres1[j], x_dram[own_rows[j]:own_rows[j] + 128, :], xres1_d[j],
                                            writes=[xres1_r[j]])))
            if halo_rows is not None:
                for j, sl in ((0, 0), (1, 3)):
                    fitems[sl] = FrontItem(
                        xtmp[j], xtmp_r[j], setc, 16, 0, (lambda kc, sl=sl: uT[:, sl, kc, :]), r_uT[sl],
                        load=(lambda j=j: P.dma("sync", xtmp[j], x_dram[halo_rows[j]:halo_rows[j] + 128, :], xtmp_d[j],
                                                writes=[xtmp_r[j]])))
            return fitems

        def prefetch(spec):
            for sl, lt in enumerate(spec["ut_tiles"]):
                P.dma("sync", uT[:, sl, :, :], uts_rows(lt), uTld_d[sl], reads=[r_uTs], writes=[r_uT[sl]])
            for j in range(2):
                ap, rr, dd = spec["ob"][j]
                r0 = spec["own_rows"][j]
                P.dma("sync", ap, spec["x"][r0:r0 + 128, :], dd, writes=[rr])

        def group1(spec, fitems=None, next_spec=None):
            setc, rope_tiles, n_kc, band_cfg = spec["setc"], spec["rope_tiles"], spec["n_kc"], spec["band_cfg"]
            x1_row0, own_kv = spec["x1_row0"], spec["own_kv"]
            ob = spec["ob"]
            if fitems is None and not spec["sample"]:
                fitems = make_fronts(spec)
                order = [1, 2] + [sl for sl in (0, 3) if sl in fitems]
                pipeline([fitems[sl] for sl in order], [fr_A, fr_B, fr_C])
            slots = spec["pproj_slots"]
            for sp in range(2):
                pre = spec.get("pre", {}).get(sp)
                sv, sr = pre if pre is not None else slab_from_scratch(0, 2 + sp)
                for sl in slots:
                    bank = 4 + (pj_i[0] % 2)
                    pj_i[0] += 1
                    proj_tok(sl, sv, sr, bank)
                    P.op("act", lambda e, sl=sl, sp=sp, bank=bank: e.copy(out=pTok[:, sl, sp * 512:(sp + 1) * 512], in_=PSB[bank]),
                         reads=[PSR[bank]], swrites=[r_pTok])
            units = []
            if own_kv is not None:
                sg = lazy_slab(0, 4)
                for j in range(2):
                    units.append(make_kv_unit(1 + j, j * 128, j, None, sg, out_rows=own_kv + j * 128))
            for sq in range(2):
                sg = lazy_slab(0, sq)
                for j in range(2):
                    u = QK()
                    u.nh, u.gain_t, u.gain_r = 4, qg_t, res("qg")
                    u.rope = (lambda j=j: load_rope(rope_tiles[j])) if rope_tiles is not None else None
                    u.tbank = 6
                    u.dst_r = r_QT
                    u.kout = None

                    def dst_fn(e, src, sq=sq, j=j):
                        return e.copy(out=QT.rearrange("p (h n) -> p h n", n=256)[:, 4 * sq:4 * sq + 4, j * 128:(j + 1) * 128],
                                      in_=src.rearrange("p (h n) -> p h n", n=128))
                    u.dst_fn = dst_fn

                    def D_stage(u=u, j=j, sg=sg):
                        u.bank = 4 + (pj_i[0] % 2)
                        pj_i[0] += 1
                        sv, sr = sg()
                        proj_tok(1 + j, sv, sr, u.bank)
                    u.D = D_stage
                    units.append(u)
            pipeline(units, [lambda u: u.D(), qk_E, qk_F])
            for j in range(2):
                sl = 1 + j
                for half in range(2):
                    bank = 0 + 2 * j + half
                    for q4 in range(4):
                        a = half * 4 + q4
                        w, cc = a // 2, a % 2
                        cfg = band_cfg[sl]
                        for n, (src_sl, kind) in enumerate(cfg):
                            P.op("pe", lambda e, q4=q4, w=w, cc=cc, src_sl=src_sl, kind=kind, n=n, bank=bank, cfg=cfg: e.matmul(
                                PSB[bank][:, q4 * 128:(q4 + 1) * 128],
                                lhsT=pTok[:, src_sl, w * 256 + cc * 128:w * 256 + (cc + 1) * 128],
                                rhs=band(kind, w), start=(n == 0), stop=(n == len(cfg) - 1)),
                                reads=[r_pTok, res("bands")], writes=[PSR[bank]], inc=(q4 == 3 and n == len(cfg) - 1))
                    P.op("act", lambda e, half=half, j=j, bank=bank: e.copy(
                        out=pooledT[:, half * 4:half * 4 + 4, j * 128:(j + 1) * 128],
                        in_=PSB[bank].rearrange("p (a n) -> p a n", n=128)),
                        reads=[PSR[bank]], swrites=[r_pooled])
            for w in range(4):
                bank = 4 + (w % 2)
                for ec in range(2):
                    for cc in range(2):
                        P.op("pe", lambda e, w=w, ec=ec, cc=cc, bank=bank: e.matmul(
                            PSB[bank][:, ec * 256:(ec + 1) * 256], lhsT=wp[:, w * 2 + cc, ec * 128:(ec + 1) * 128],
                            rhs=pooledT[:, w * 2 + cc, :], start=(cc == 0), stop=(cc == 1)),
                            reads=[res("wp"), r_pooled], writes=[PSR[bank]], inc=(ec == 1 and cc == 1))
                for ec in range(2):
                    P.op("act", lambda e, w=w, ec=ec, bank=bank: e.activation(
                        out=mixT[:, 8 + w * 2 + ec, :], in_=PSB[bank][:, ec * 256:(ec + 1) * 256],
                        func=AF.Identity, bias=0.0, scale=psT[:, w * 2 + ec:w * 2 + ec + 1]),
                        reads=[PSR[bank], res("psT")], swrites=[r_mixP])
            mixf = mixT.rearrange("p k n -> p (k n)")

            def S_mm(h2, kc):
                sb = kc % 2
                for jj in range(2):
                    P.op("pe", lambda e, kc=kc, sb=sb, jj=jj, h2=h2: e.matmul(
                        PSB[2 * sb + jj], lhsT=KT[:, h2, kc * 128:(kc + 1) * 128],
                        rhs=QT[:, (4 * h2 + 2 * jj) * 256:(4 * h2 + 2 * jj + 2) * 256], start=True, stop=True),
                        reads=[r_KT, r_QT], writes=[PSR[2 * sb + jj]], inc=(jj == 1))

            mod_pre = None
            if spec.get("mod_s") is not None:
                mod_pre = slab_load(wslab(w_mod, spec["mod_s"] * 512))
            seq = [(h2, kc) for h2 in range(2) for kc in range(n_kc)]

            def S_i(i):
                h2, kc = seq[i]
                sb = i % 2
                for jj in range(2):
                    P.op("pe", lambda e, kc=kc, sb=sb, jj=jj, h2=h2: e.matmul(
                        PSB[2 * sb + jj], lhsT=KT[:, h2, kc * 128:(kc + 1) * 128],
                        rhs=QT[:, (4 * h2 + 2 * jj) * 256:(4 * h2 + 2 * jj + 2) * 256], start=True, stop=True),
                        reads=[r_KT, r_QT], writes=[PSR[2 * sb + jj]], inc=(jj == 1))

            S_i(0)
            S_i(1)
            for i, (h2, kc) in enumerate(seq):
                sb = i % 2
                pb = i % 3
                P.op("act", lambda e, sb=sb, pb=pb: e.activation(out=PTe[pb], in_=ps2(2 * sb), func=AF.Exp,
                                                                 bias=eps_t[:, 1:2], scale=ATT_SCALE),
                     reads=[PSR[2 * sb], PSR[2 * sb + 1], res("eps_t")], writes=[res("PTe%d" % pb)])
                if i + 2 < len(seq):
                    S_i(i + 2)
                for jj in range(2):
                    P.op("pe", lambda e, h2=h2, kc=kc, pb=pb, jj=jj: e.matmul(
                        PSB[4 + jj], lhsT=Vb[:, kc, h2 * 128:(h2 + 1) * 128],
                        rhs=PTe[pb][:, jj * 512:(jj + 1) * 512], start=(kc == 0), stop=(kc == n_kc - 1)),
                        reads=[r_V, res("PTe%d" % pb)], writes=[PSR[4 + jj]], inc=False)
                for jj in range(2):
                    P.op("pe", lambda e, kc=kc, pb=pb, jj=jj: e.matmul(
                        PSB[6 + jj], lhsT=ones_b, rhs=PTe[pb][:, jj * 512:(jj + 1) * 512],
                        start=(kc == 0), stop=(kc == n_kc - 1)),
                        reads=[res("ones_b"), res("PTe%d" % pb)], writes=[PSR[6 + jj]], inc=(jj == 1))
                if kc == n_kc - 1:
                    P.op("dve", lambda e: e.reciprocal(out=acc, in_=ps2(6)), reads=[PSR[6], PSR[7]], writes=[res("acc")])
                    for jj in range(2):
                        c0 = (4 * h2 + 2 * jj) * 256
                        P.op("dve", lambda e, jj=jj, c0=c0: e.tensor_tensor(
                            out=mixf[:, c0:c0 + 512], in0=PSB[4 + jj], in1=acc[:, jj * 512:(jj + 1) * 512], op=ALU.mult),
                            reads=[PSR[4 + jj], res("acc")], swrites=[r_mixO])
            if mod_pre is not None:
                mod_slabs(spec["mod_s"], spec["mod_s"] + 1, fixed_bank=7, preloaded=mod_pre)
            nfit = None
            if next_spec is not None and next_spec["sample"]:
                prefetch(next_spec)
            elif next_spec is not None:
                nfit = make_fronts(next_spec)
                halos = [nfit[sl] for sl in (0, 3) if sl in nfit]
                if halos:
                    pipeline(halos, [fr_A, fr_B, fr_C])
            for ds in range(4):
                sv, sr = slab_from_scratch(0, 5 + ds)
                for dc in range(4):
                    bank = 4 + (dc % 2)
                    for kc in range(16):
                        P.op("pe", lambda e, dc=dc, kc=kc, sv=sv, bank=bank: e.matmul(
                            PSB[bank][:, 0:256], lhsT=sv[:, kc, dc * 128:(dc + 1) * 128], rhs=mixT[:, kc, :],
                            start=(kc == 0), stop=(kc == 15)),
                            reads=[sr, r_mixO, r_mixP], writes=[PSR[bank]], inc=(kc == 15))
                    P.op("act", lambda e, dc=dc, ds=ds, bank=bank: e.activation(
                        out=fT1[:, dc, :], in_=PSB[bank][:, 0:256], func=AF.Identity, bias=0.0,
                        scale=modT[:, 32 + ds * 4 + dc, setc:setc + 1]),
                        reads=[PSR[bank], res("modTb")], swrites=[res("fT1")])
                for j in range(2):
                    bank = 2 * (ds % 2) + j
                    for dc in range(4):
                        P.op("pe", lambda e, dc=dc, j=j, bank=bank: e.transpose(
                            out=PSB[bank][:, dc * 128:(dc + 1) * 128], in_=fT1[:, dc, j * 128:(j + 1) * 128],
                            identity=ident_f), reads=[res("fT1"), res("ident_f")], writes=[PSR[bank]], inc=(dc == 3))
                    P.op("dve", lambda e, j=j, ds=ds, bank=bank: e.scalar_tensor_tensor(
                        out=ob[j][0][:, ds * 512:(ds + 1) * 512], in0=ob[j][0][:, ds * 512:(ds + 1) * 512],
                        scalar=ALPHA, in1=PSB[bank], op0=ALU.mult, op1=ALU.add),
                        reads=[PSR[bank], ob[j][1]], writes=[ob[j][1]])
            if next_spec is not None and next_spec["sample"]:
                next_spec["pre"] = {sp: slab_from_scratch(0, 2 + sp) for sp in range(2)}
            def ln_stage(j):
                ln_affine_store(ob[j][0], ob[j][1], x1s[x1_row0 + j * 128:x1_row0 + (j + 1) * 128, :], ob[j][2])
            if nfit is None:
                for j in range(2):
                    ln_stage(j)
            else:
                pipeline([0, 1], [ln_stage, lambda j: fr_load(nfit[1 + j]), lambda j: fr_A(nfit[1 + j]),
                                  lambda j: fr_B(nfit[1 + j]), lambda j: fr_C(nfit[1 + j])])
            return nfit

        OB = [[(xres1[j], xres1_r[j], xres1_d[j]) for j in range(2)],
              [(xtmp[j], xtmp_r[j], xtmp_d[j]) for j in range(2)]]
        specs = []
        for g in range(8):
            t0 = 2 * g
            cfg = {
                1: [(0, "Bm1_fo" if g == 0 else "Bm1_std"), (1, "B0_fo" if g == 0 else "B0_mid"), (2, "Bp1_std")],
                2: [(1, "Bm1_std"), (2, "B0_lo" if g == 7 else "B0_mid"), (3, "Bp1_lo" if g == 7 else "Bp1_std")],
            }
            specs.append(dict(x=xs, own_rows=[t0 * 128, (t0 + 1) * 128],
                              halo_rows=[((t0 - 1) % 32) * 128, (t0 + 2) * 128], setc=0, rope_tiles=[t0, t0 + 1],
                              n_kc=36, band_cfg=cfg, x1_row0=g * 256, own_kv=None, mod_s=14 + g,
                              sample=True, ob=OB[g % 2], pproj_slots=[0, 1, 2, 3],
                              ut_tiles=[(t0 - 1) % 32, t0, t0 + 1, t0 + 2]))
        for sq_ in range(2):
            cfg = {1: [(1, "B0_pf"), (2, "Bp1_std")], 2: [(1, "Bm1_std"), (2, "B0_pl")]}
            specs.append(dict(x=xp, own_rows=[sq_ * 256, sq_ * 256 + 128], halo_rows=None, setc=1, rope_tiles=None,
                              n_kc=2, band_cfg=cfg, x1_row0=2048 + sq_ * 256, own_kv=sq_ * 256, mod_s=22 + sq_,
                              sample=False, ob=OB[0], pproj_slots=[1, 2]))
        prefetch(specs[0])
        fit = None
        for g, spec in enumerate(specs):
            if g < 8:
                for cidx, csrc in convB[4 * g:4 * g + 4]:
                    convert(1, cidx, csrc)
            if g == 8:
                conv_done(1)
            fit = group1(spec, fit, specs[g + 1] if g + 1 < len(specs) else None)
        mod_plus1(64)

        P.barrier()
        c2 = Carver(big, pers_end)
        slab3 = c2.bf16(16 * 512)
        xres2 = [c2.f32(D) for _ in range(4)]
        uT2 = c2.bf16(16 * 512).rearrange("p (k n) -> p k n", n=512)
        hT = c2.bf16(64 * 512).rearrange("p (k n) -> p k n", n=512)
        fT2 = c2.f32(4 * 512).rearrange("p (c n) -> p c n", n=512)
        rtmp = [c2.f32(512) for _ in range(2)]
        spare = c2.f32(D)
        print('c2.off', c2.off)
        assert c2.off <= NW, c2.off
        slab_bufs.append(slab3)
        slab_state["n"] = 3
        xres2_r = [res("x2_%d" % t) for t in range(4)]
        xres2_d = [xres1_d[0], xres1_d[1], xtmp_d[0], xtmp_d[1]]
        r_uT2 = [res("uT2_%d" % i) for i in range(4)]
        r_hT, r_fT2 = res("hT"), res("fT2")
        rt_r = [res("rtmp0"), res("rtmp1")]

        P.dma("sync", tab_g, ln2_g.to_broadcast([128, D]), tab_d, writes=[res("tab_g")])
        P.dma("sync", tab_b, ln2_b.to_broadcast([128, D]), tab_d2, writes=[res("tab_b")])

        spare_r, spare_d = res("spare"), new_dsem()

        def front2_item(g, t):
            setc = 0 if g < 4 else 1
            row = g * 512 + t * 128
            return FrontItem(spare, spare_r, setc, 64, 48, (lambda kc, t=t: uT2[:, kc, t * 128:(t + 1) * 128]), r_uT2[t],
                             load=(lambda row=row: P.dma("sync", spare, x1s[row:row + 128, :], spare_d, writes=[spare_r])))

        def load_x1(g, t):
            row = g * 512 + t * 128
            P.dma("sync", xres2[t], x1s[row:row + 128, :], xres2_d[t], writes=[xres2_r[t]])

        def ln2_stage(g, t):
            row = g * 512 + t * 128
            dst = ys[row:row + 128, :] if g < 4 else yp[t * 128:(t + 1) * 128, :]
            ln_affine_store(xres2[t], xres2_r[t], dst, xres2_d[t])

        for t in range(4):
            load_x1(0, t)
        pipeline([front2_item(0, t) for t in range(4)], [fr_A, fr_B, fr_C])
        for g in range(5):
            setc = 0 if g < 4 else 1
            n_ev = 0
            for s in range(16):
                sv, sr = slab_from_scratch(1, N_SLAB_A + s)
                for j in range(4):
                    bank = 4 + ((s * 4 + j) % 4)
                    for kc in range(16):
                        P.op("pe", lambda e, j=j, kc=kc, sv=sv, bank=bank: e.matmul(
                            PSB[bank], lhsT=sv[:, kc, j * 128:(j + 1) * 128], rhs=uT2[:, kc, :],
                            start=(kc == 0), stop=(kc == 15)),
                            reads=[sr] + r_uT2, writes=[PSR[bank]], inc=(kc == 15))
                    ri = n_ev % 2
                    n_ev += 1
                    P.op("act", lambda e, bank=bank, ri=ri: e.activation(out=rtmp[ri], in_=PSB[bank], func=AF.Relu),
                         reads=[PSR[bank]], writes=[rt_r[ri]])
                    P.op("dve", lambda e, s=s, j=j, ri=ri: e.tensor_tensor(out=hT[:, s * 4 + j, :], in0=rtmp[ri], in1=rtmp[ri],
                                                                         op=ALU.mult),
                         reads=[rt_r[ri]], swrites=[r_hT])
                if g > 0 and s % 2 == 1 and s // 2 < 4:
                    t = s // 2
                    ln2_stage(g - 1, t)
                    load_x1(g, t)
            nfi = [front2_item(g + 1, t) for t in range(4)] if g < 4 else None
            for ds in range(4):
                for kq in range(4):
                    sv, sr = slab_from_scratch(1, N_SLAB_A + 16 + ds * 4 + kq)
                    for dc in range(4):
                        for kc in range(16):
                            P.op("pe", lambda e, dc=dc, kc=kc, kq=kq, sv=sv: e.matmul(
                                PSB[4 + dc], lhsT=sv[:, kc, dc * 128:(dc + 1) * 128], rhs=hT[:, kq * 16 + kc, :],
                                start=(kq == 0 and kc == 0), stop=(kq == 3 and kc == 15)),
                                reads=[sr, r_hT], writes=[PSR[4 + dc]], inc=(kc == 15))
                    if nfi is not None and kq == 1:
                        fr_A(nfi[ds])
                    if nfi is not None and kq == 2:
                        fr_B(nfi[ds])
                        fr_C(nfi[ds])
                for dc in range(4):
                    P.op("act", lambda e, dc=dc, ds=ds, setc=setc: e.activation(
                        out=fT2[:, dc, :], in_=PSB[4 + dc], func=AF.Identity, bias=0.0,
                        scale=modT[:, 80 + ds * 4 + dc, setc:setc + 1]),
                        reads=[PSR[4 + dc], res("modTc")], swrites=[r_fT2])
                for t in range(4):
                    bank = t
                    for dc in range(4):
                        P.op("pe", lambda e, dc=dc, t=t, bank=bank: e.transpose(
                            out=PSB[bank][:, dc * 128:(dc + 1) * 128], in_=fT2[:, dc, t * 128:(t + 1) * 128],
                            identity=ident_f), reads=[r_fT2, res("ident_f")], writes=[PSR[bank]], inc=(dc == 3))
                    P.op("dve", lambda e, t=t, ds=ds, bank=bank: e.scalar_tensor_tensor(
                        out=xres2[t][:, ds * 512:(ds + 1) * 512], in0=xres2[t][:, ds * 512:(ds + 1) * 512],
                        scalar=ALPHA, in1=PSB[bank], op0=ALU.mult, op1=ALU.add),
                        reads=[PSR[bank], xres2_r[t]], writes=[xres2_r[t]])
        for t in range(4):
            ln2_stage(4, t)

        P.op("pe", lambda e: e.transpose(out=PSB[7][:, 0:8], in_=ident_f[0:8, :], identity=ident_f[0:8, 0:8]),
             reads=[], writes=[PSR[7]], inc=True)
        P.barrier()

        _DBG['engs'] = engs
        with nc.Block() as block:
            @block.tensor
            def _(e):
                emit(e, engs["pe"])

            @block.scalar
            def _(e):
                emit(e, engs["act"])

            @block.vector
            def _(e):
                emit(e, engs["dve"])

            @block.gpsimd
            def _(e):
                emit(e, engs["pool"])

            @block.sync
            def _(e):
                emit(e, engs["sync"])
    return nc


_DBG = {}


def _rope_tables(hf):
    inv = (10000.0 ** (-(np.arange(32, dtype=np.float32) / 32.0))).astype(np.float32)
    out = np.zeros((32, 128, 256), np.float32)
    for lt in range(32):
        gt = (lt + hf * 16) % 32
        tok = gt * 128 + np.arange(128)
        row = (tok // 64).astype(np.float32)
        col = (tok % 64).astype(np.float32)
        ar = (row[:, None] * inv[None, :]).astype(np.float32)
        ac = (col[:, None] * inv[None, :]).astype(np.float32)
        cr, sr, cc, sc = np.cos(ar), np.sin(ar), np.cos(ac), np.sin(ac)
        out[lt, :, 0:128] = np.concatenate([cr, cr, cc, cc], axis=1)
        out[lt, :, 128:256] = np.concatenate([-sr, sr, -sc, sc], axis=1)
    return out.reshape(32 * 128, 256)


def _band(w, kind):
    half = w // 2
    A = np.zeros((128, 128), np.float64)
    if kind == "zero":
        return A.T.astype(np.float32)
    for d in range(128):
        if kind in ("mid", "first", "last"):
            lo, hi = d - half, d + half
            if kind == "first":
                cnt = hi - max(lo, 0)
            elif kind == "last":
                cnt = min(hi, 128) - lo
            else:
                cnt = w
            for s in range(max(lo, 0), min(hi, 128)):
                A[d, s] += 1.0 / cnt
            A[d, d] -= 1.0
        elif kind == "m1":
            for s in range(128):
                if s - 128 >= d - half:
                    A[d, s] += 1.0 / w
        elif kind == "p1":
            for s in range(128):
                if 128 + s <= d + half - 1:
                    A[d, s] += 1.0 / w
    return A.T.astype(np.float32)


def _bands(hf):
    kinds = [
        "mid", "m1", "p1",
        "first" if hf == 0 else "mid",
        "zero" if hf == 0 else "m1",
        "mid" if hf == 0 else "last",
        "p1" if hf == 0 else "zero",
        "first", "last",
    ]
    mats = np.zeros((128, 36, 128), np.float32)
    for ki, k in enumerate(kinds):
        for wi, w in enumerate((2, 4, 8, 16)):
            mats[:, ki * 4 + wi, :] = _band(w, k)
    return mats.reshape(128, 36 * 128)


_NC_CACHE = {}
_DBG = {}


def kernel(x_prompt, x_sample, cache_k, cache_v, c, c_ctx, w_mod, b_mod, w_in, q_gain, k_gain,
           w_pool, pool_scale, w_out, ln1_g, ln1_b, w_ff1, w_ff2, ln2_g, ln2_b):
    f = lambda a: np.ascontiguousarray(np.asarray(a, dtype=np.float32))
    x_prompt, x_sample, cache_k, cache_v, c, c_ctx = map(f, (x_prompt, x_sample, cache_k, cache_v, c, c_ctx))
    if "nc" not in _NC_CACHE:
        _NC_CACHE["nc"] = build_program()
    nc = _NC_CACHE["nc"]
    shared = {
        "w_mod": f(w_mod)[0], "b_mod": f(b_mod)[0].reshape(96, 128), "w_in": f(w_in)[0],
        "qg": f(q_gain)[0].reshape(1, 128), "kg": f(k_gain)[0].reshape(1, 128),
        "w_pool": f(w_pool)[0].reshape(1024, 256), "pool_scale": f(pool_scale)[0].reshape(8, 128),
        "w_out": f(w_out)[0], "ln1_g": f(ln1_g)[0].reshape(1, D), "ln1_b": f(ln1_b)[0].reshape(1, D),
        "ln2_g": f(ln2_g)[0].reshape(1, D), "ln2_b": f(ln2_b)[0].reshape(1, D),
        "w_ff1": f(w_ff1)[0], "w_ff2": f(w_ff2)[0], "ident": np.eye(128, dtype=np.float32),
    }
    in_maps = []
    for i in range(N_CORES):
        b, hf = i // 2, i % 2
        xb = x_sample[b]
        xs = np.ascontiguousarray(np.concatenate([xb[hf * 2048:(hf + 1) * 2048], xb[(1 - hf) * 2048:(2 - hf) * 2048]], axis=0))
        m = dict(shared)
        m["xs"] = xs
        m["xp"] = np.ascontiguousarray(x_prompt[2 * i:2 * i + 2].reshape(512, D))
        m["ck"] = np.ascontiguousarray(cache_k[b, 0].reshape(512, 256))
        m["cv"] = np.ascontiguousarray(cache_v[b, 0].reshape(512, 256))
        m["cvec"] = np.ascontiguousarray(np.stack([c[b], c_ctx], axis=0).reshape(32, 128))
        m["rope"] = _rope_tables(hf)
        m["bands"] = _bands(hf)
        in_maps.append(m)
    res = run_bass_kernel_spmd(nc, in_maps, core_ids=list(range(N_CORES)))
    rs = res.results
    if DEBUG:
        _DBG["rs"] = rs
    y_sample = np.zeros((4, 4096, D), np.float32)
    y_prompt = np.zeros((16, 256, D), np.float32)
    ctx_k = np.zeros((16, 1, 256, 2, 128), np.float32)
    ctx_v = np.zeros((16, 1, 256, 2, 128), np.float32)
    for i in range(N_CORES):
        b, hf = i // 2, i % 2
        y_sample[b, hf * 2048:(hf + 1) * 2048] = rs[i]["ys"]
        y_prompt[2 * i:2 * i + 2] = rs[i]["yp"].reshape(2, 256, D)
        ctx_k[2 * i:2 * i + 2, 0] = rs[i]["cko"].reshape(2, 256, 2, 128)
        ctx_v[2 * i:2 * i + 2, 0] = rs[i]["cvo"].reshape(2, 256, 2, 128)
    return (y_prompt, y_sample, ctx_k, ctx_v)
```

```python
import numpy as np
import concourse.bass as bass
import concourse.mybir as mybir
from concourse.bass_utils import run_bass_kernel_spmd

F32 = mybir.dt.float32
BF16 = mybir.dt.bfloat16
AF = mybir.ActivationFunctionType
ALU = mybir.AluOpType

D = 2048
EPS = 1e-6
ALPHA = float(2.0 ** 0.25)
ATT_SCALE = float(128 ** -0.5)
EXP_SHIFT = -10.0
N_CORES = 8
SAME_ENGINE_WAITS = True
DEBUG = False


class Res:
    __slots__ = ("name", "w", "r")

    def __init__(self, name):
        self.name = name
        self.w = {}
        self.r = {}

    def set_w(self, tok):
        self.w = {tok[0]: (tok[1], False)}
        self.r = {}


class DSem:
    def __init__(self, sem):
        self.sem = sem
        self.cnt = 0


class Eng:
    def __init__(self, name, sem):
        self.name = name
        self.sem = sem
        self.cnt = 0
        self.ops = []
        self.waited = {}


class Prog:
    def __init__(self, engs):
        self.E = engs
        self.dsems = []

    def _deps(self, E, reads, writes, swrites):
        need = {}

        def add(s, v):
            if need.get(s, 0) < v:
                need[s] = v

        for r in reads:
            for s, (v, _) in r.w.items():
                add(s, v)
            if r.name.startswith("bank"):
                for s, v in r.r.items():
                    if s != E.sem:
                        add(s, v)
        for w in writes:
            for s, (v, _) in w.w.items():
                add(s, v)
            for s, v in w.r.items():
                add(s, v)
        for w in swrites:
            for s, (v, sh) in w.w.items():
                if not sh:
                    add(s, v)
            for s, v in w.r.items():
                add(s, v)
        for s, v in need.items():
            if s == E.sem:
                if E.name == "pe" or not SAME_ENGINE_WAITS:
                    continue
                v = min(v, E.cnt)
                if v <= 0:
                    continue
            if E.waited.get(s, 0) < v:
                E.waited[s] = v
                E.ops.append(("wait", s, v))

    def _post(self, tok, reads, writes, swrites):
        s, v = tok
        for r in reads:
            if r.r.get(s, 0) < v:
                r.r[s] = v
        for w in writes:
            w.w = {s: (v, False)}
            w.r = {}
        for w in swrites:
            if w.r:
                w.w = {s: (v, True)}
                w.r = {}
            else:
                old = w.w.get(s)
                if old is None or old[1]:
                    w.w[s] = (v, True)
                else:
                    w.w[s] = (v, False)

    def op(self, eng, fn, reads=(), writes=(), swrites=(), inc=True):
        E = self.E[eng]
        self._deps(E, reads, writes, swrites)
        if inc:
            E.cnt += 1
            tok = (E.sem, E.cnt)
        else:
            tok = (E.sem, E.cnt + 1)
        E.ops.append(("op", fn, inc))
        self._post(tok, reads, writes, swrites)

    def dma(self, q, out, in_, dsem, reads=(), writes=()):
        E = self.E[q]
        self._deps(E, reads, writes, ())
        dsem.cnt += 16
        tok = (dsem.sem, dsem.cnt)
        E.ops.append(("dma", out, in_, dsem.sem))
        self._post(tok, reads, writes, ())

    def barrier(self):
        for E in self.E.values():
            for O in self.E.values():
                if O is E or O.cnt == 0:
                    continue
                if E.waited.get(O.sem, 0) < O.cnt:
                    E.waited[O.sem] = O.cnt
                    E.ops.append(("wait", O.sem, O.cnt))
            for d in self.dsems:
                if d.cnt and E.waited.get(d.sem, 0) < d.cnt:
                    E.waited[d.sem] = d.cnt
                    E.ops.append(("wait", d.sem, d.cnt))


def emit(e, E):
    for o in E.ops:
        if o[0] == "wait":
            e.wait_ge(o[1], o[2])
        elif o[0] == "op":
            ins = o[1](e)
            if o[2]:
                ins.then_inc(E.sem, 1)
        else:
            e.dma_start(out=o[1], in_=o[2]).then_inc(o[3], 16)


def pipeline(items, stages):
    n, S = len(items), len(stages)
    for step in range(n + S - 1):
        for s in reversed(range(S)):
            t = step - s
            if 0 <= t < n:
                stages[s](items[t])


class Carver:
    def __init__(self, big, base=0):
        self.big = big
        self.off = base

    def f32(self, n):
        ap = self.big[:, self.off:self.off + n]
        self.off += n
        return ap

    def bf16(self, n):
        words = (n + 1) // 2
        ap = self.big[:, self.off:self.off + words].bitcast(BF16)
        self.off += words
        return ap


def build_program():
    nc = bass.Bass("TRN2", target_bir_lowering=False)

    def din(name, shape):
        return nc.dram_tensor(name, list(shape), F32, kind="ExternalInput").ap()

    def dout(name, shape):
        return nc.dram_tensor(name, list(shape), F32, kind="ExternalOutput").ap()

    xs = din("xs", [4096, D])
    xp = din("xp", [512, D])
    ck = din("ck", [512, 256])
    cv = din("cv", [512, 256])
    cvec = din("cvec", [32, 128])
    w_mod = din("w_mod", [D, 6 * D])
    b_mod = din("b_mod", [96, 128])
    w_in = din("w_in", [D, 2560])
    qg = din("qg", [1, 128])
    kg = din("kg", [1, 128])
    w_pool = din("w_pool", [1024, 256])
    pool_scale = din("pool_scale", [8, 128])
    w_out = din("w_out", [D, D])
    ln1_g = din("ln1_g", [1, D])
    ln1_b = din("ln1_b", [1, D])
    ln2_g = din("ln2_g", [1, D])
    ln2_b = din("ln2_b", [1, D])
    w_ff1 = din("w_ff1", [D, 4 * D])
    w_ff2 = din("w_ff2", [4 * D, D])
    rope = din("rope", [32 * 128, 256])
    bands = din("bands", [128, 36 * 128])
    ident = din("ident", [128, 128])

    ys = dout("ys", [2048, D])
    yp = dout("yp", [512, D])
    cko = dout("cko", [512, 256])
    cvo = dout("cvo", [512, 256])
    x1s = nc.dram_tensor("x1s", [2560, D], F32, kind="Internal").ap()
    N_SLAB_A, N_SLAB_B = 9, 32
    wsc = nc.dram_tensor("wsc", [(N_SLAB_A + N_SLAB_B) * 128, 8192], BF16, kind="Internal").ap()

    uTs = nc.dram_tensor("uTs", [18 * 128, 2048], BF16, kind="Internal").ap()
    NW = 53200
    from contextlib import ExitStack
    with ExitStack() as _stk:
        big_t = _stk.enter_context(nc.sbuf_tensor("big", [128, NW], F32))
        ps_t = _stk.enter_context(nc.psum_tensor("ps", [128, 4096], F32))
        s_pe, s_act, s_dve, s_pool, s_sync = [_stk.enter_context(nc.semaphore(n)) for n in
                                              ("s_pe", "s_act", "s_dve", "s_pool", "s_sync")]
        _dl = [_stk.enter_context(nc.semaphore("d%d" % i)) for i in range(38)]
        big = big_t[:, :]
        psa = ps_t[:, :]
        engs = {
            "pe": Eng("pe", s_pe), "act": Eng("act", s_act), "dve": Eng("dve", s_dve),
            "pool": Eng("pool", s_pool), "sync": Eng("sync", s_sync),
        }
        P = Prog(engs)
        dpool = [DSem(s) for s in _dl]
        P.dsems = dpool
        dnext = [0]

        def new_dsem():
            d = dpool[dnext[0]]
            dnext[0] += 1
            return d

        PSB = [psa[:, b * 512:(b + 1) * 512] for b in range(8)]
        PSBb = [psa[:, b * 512:(b + 1) * 512].bitcast(BF16) for b in range(8)]
        PSR = [Res("bank%d" % b) for b in range(8)]

        def ps2(b0):
            return psa[:, b0 * 512:(b0 + 2) * 512]

        cvr = Carver(big)
        ident_f = cvr.f32(128)
        ident_b = cvr.bf16(128)
        ones_b = cvr.bf16(128)
        modT = cvr.f32(192).rearrange("p (m s) -> p m s", s=2)
        scT = cvr.bf16(32)
        bmT = cvr.f32(96)
        psT = cvr.f32(8)
        eps_t = cvr.f32(8)
        _stg = Carver(big, 60000)
        mvs = [cvr.f32(8) for _ in range(4)]
        sts = [cvr.f32(24) for _ in range(4)]
        ssb = [cvr.f32(8) for _ in range(4)]
        tab_g = cvr.f32(D)
        tab_b = cvr.f32(D)
        slab_bufs = [cvr.bf16(16 * 512) for _ in range(2)]
        xhats = [cvr.bf16(D) for _ in range(2)]
        pers_end = cvr.off
        _stg = Carver(big, pers_end + 30000)
        cv_sb = _stg.f32(128)
        bm_sb = _stg.f32(128)
        ps_sb = _stg.f32(128)

        R = {}

        def res(name):
            if name not in R:
                R[name] = Res(name)
            return R[name]

        slab_res = [res("slab0"), res("slab1"), res("slab2")]
        slab_ds = [new_dsem(), new_dsem(), new_dsem()]
        slab_ds_hw = [new_dsem(), new_dsem(), new_dsem()]
        slab_state = {"n": 2, "i": 0}
        mv_i = [0]

        def slab_view(i):
            return slab_bufs[i].rearrange("p (k n) -> p k n", n=512)

        def slab_load(src3d, buf=None):
            if buf is None:
                i = slab_state["i"] % slab_state["n"]
                slab_state["i"] += 1
            else:
                i = buf
            v = slab_view(i)
            P.dma("pool", v, src3d, slab_ds[i], writes=[slab_res[i]])
            return v, slab_res[i]

        def wslab(w, c0, k0=0):
            return w.rearrange("(k p) n -> p k n", p=128)[:, k0:k0 + 16, c0:c0 + 512]

        def scr(idx):
            return wsc[idx * 128:(idx + 1) * 128, :].rearrange("p (k n) -> p k n", n=512)

        conv_d = [[new_dsem(), new_dsem()], [new_dsem(), new_dsem(), new_dsem(), new_dsem()]]
        conv_n = [0, 0]
        r_conv = [res("convA"), res("convB")]

        def convert(batch, idx, src3d):
            E = engs["pool"]
            ds = conv_d[batch]
            d = ds[conv_n[batch] % len(ds)]
            conv_n[batch] += 1
            if d.cnt > 0 and E.waited.get(d.sem, 0) < d.cnt:
                E.waited[d.sem] = d.cnt
                E.ops.append(("wait", d.sem, d.cnt))
            P.dma("pool", scr(idx), src3d, d)

        def conv_done(batch):
            r_conv[batch].w = {d.sem: (d.cnt, False) for d in conv_d[batch] if d.cnt}

        def slab_from_scratch(batch, idx):
            i = slab_state["i"] % slab_state["n"]
            slab_state["i"] += 1
            v = slab_view(i)
            P.dma("sync", v, scr(idx), slab_ds_hw[i], reads=[r_conv[batch]], writes=[slab_res[i]])
            return v, slab_res[i]

        misc_d = new_dsem()

        P.dma("sync", ident_f, ident, misc_d, writes=[res("ident_f")])
        P.dma("sync", cv_sb[0:32, :], cvec, misc_d, writes=[res("cv_sb")])
        P.dma("sync", bm_sb[0:96, :], b_mod, misc_d, writes=[res("bm_sb")])
        P.dma("sync", ps_sb[0:8, :], pool_scale, misc_d, writes=[res("ps_sb")])
        for _n in ("ident_f", "cv_sb", "bm_sb", "ps_sb"):
            res(_n).w = {misc_d.sem: (misc_d.cnt, False)}
        P.op("dve", lambda e: e.tensor_copy(out=ident_b, in_=ident_f),
             reads=[res("ident_f")], writes=[res("ident_b")])
        P.op("dve", lambda e: e.memset(ones_b, 1.0), writes=[res("ones_b")])
        P.op("dve", lambda e: e.memset(eps_t, EPS), writes=[res("eps_t")])
        P.op("dve", lambda e: e.memset(eps_t[:, 1:2], EXP_SHIFT), writes=[res("eps_t")])
        P.op("act", lambda e: e.activation(out=cv_sb[0:32, :], in_=cv_sb[0:32, :], func=AF.Silu),
             reads=[res("cv_sb")], writes=[res("cv_sb")])
        P.op("pe", lambda e: e.transpose(out=PSB[0][:, 0:32], in_=cv_sb[0:32, :], identity=ident_f[0:32, 0:32]),
             reads=[res("cv_sb"), res("ident_f")], writes=[PSR[0]])
        P.op("dve", lambda e: e.tensor_copy(out=scT, in_=PSB[0][:, 0:32]), reads=[PSR[0]], writes=[res("scT")])
        P.op("pe", lambda e: e.transpose(out=PSB[1][:, 0:96], in_=bm_sb[0:96, :], identity=ident_f[0:96, 0:96]),
             reads=[res("bm_sb"), res("ident_f")], writes=[PSR[1]])
        P.op("dve", lambda e: e.tensor_copy(out=bmT, in_=PSB[1][:, 0:96]), reads=[PSR[1]], writes=[res("bmT")])
        P.op("pe", lambda e: e.transpose(out=PSB[2][:, 0:8], in_=ps_sb[0:8, :], identity=ident_f[0:8, 0:8]),
             reads=[res("ps_sb"), res("ident_f")], writes=[PSR[2]])
        P.op("dve", lambda e: e.tensor_copy(out=psT, in_=PSB[2][:, 0:8]), reads=[PSR[2]], writes=[res("psT")])

        scT3 = scT.rearrange("p (s k) -> p s k", k=16)

        def mod_slabs(s0, s1, buf=None, fixed_bank=None, preloaded=None):
            for s in range(s0, s1):
                if preloaded is None:
                    sv, sr = slab_load(wslab(w_mod, s * 512), buf=buf)
                else:
                    sv, sr = preloaded
                bank = (6 + (s % 2)) if fixed_bank is None else fixed_bank
                for j in range(4):
                    for kc in range(16):
                        P.op("pe", lambda e, j=j, kc=kc, sv=sv, bank=bank: e.matmul(
                            PSB[bank][:, 2 * j:2 * j + 2], lhsT=sv[:, kc, j * 128:(j + 1) * 128],
                            rhs=scT3[:, :, kc], start=(kc == 0), stop=(kc == 15)),
                            reads=[sr, res("scT")], writes=[PSR[bank]], inc=(kc == 15 and j == 3))
                P.op("dve", lambda e, s=s, bank=bank: e.tensor_tensor(
                    out=modT[:, 4 * s:4 * s + 4, :],
                    in0=PSB[bank][:, 0:8].rearrange("p (j s) -> p j s", s=2),
                    in1=bmT[:, 4 * s:4 * s + 4].unsqueeze(2).to_broadcast([128, 4, 2]), op=ALU.add),
                    reads=[PSR[bank], res("bmT")], swrites=[res("modTa" if s < 8 else ("modTb" if s < 12 else "modTc"))])

        def mod_plus1(m0):
            rr = res("modTa" if m0 < 32 else "modTc")
            P.op("dve", lambda e: e.tensor_scalar_add(out=modT[:, m0:m0 + 16, :], in0=modT[:, m0:m0 + 16, :], scalar1=1.0),
                 reads=[rr], writes=[rr])

        mod_slabs(0, 8)
        mod_plus1(16)

        c1 = Carver(big, pers_end)
        KT = c1.bf16(2 * 4608).rearrange("p (h n) -> p h n", n=4608)
        Vb = c1.bf16(36 * 256).rearrange("p (c n) -> p c n", n=256)
        qg_t = c1.f32(128)
        kg_t = c1.f32(128)
        bands_sb = c1.bf16(36 * 128).rearrange("p (m n) -> p m n", n=128)
        wp = c1.bf16(8 * 256).rearrange("p (a e) -> p a e", e=256)
        rope_b = [c1.f32(256) for _ in range(2)]
        _xres_off = c1.off
        xres1 = [c1.f32(D) for _ in range(2)]
        kvslab = big[:, _xres_off:_xres_off + 2 * D].bitcast(BF16).rearrange("p (k n) -> p k n", n=512)
        xtmp = [c1.f32(D) for _ in range(2)]
        uT = c1.bf16(16 * 512).rearrange("p (s k n) -> p s k n", s=4, k=16)
        mixT = c1.bf16(16 * 256).rearrange("p (k n) -> p k n", n=256)
        QT = c1.bf16(8 * 256)
        _ptok_off = c1.off
        pTok = c1.bf16(4 * 1024).rearrange("p (t n) -> p t n", n=1024)
        acc2 = big[:, _ptok_off:_ptok_off + 1024]
        pooledT = c1.bf16(8 * 256).rearrange("p (a n) -> p a n", n=256)
        kn = c1.f32(512)
        t1 = c1.f32(512)
        t2 = c1.f32(512)
        qrs = [c1.bf16(512) for _ in range(2)]
        vout = c1.f32(256)
        PTe = [c1.bf16(1024) for _ in range(3)]
        acc = c1.f32(1024)
        fT1 = c1.f32(4 * 256).rearrange("p (c n) -> p c n", n=256)
        print('c1.off', c1.off)
        assert c1.off <= NW, c1.off

        xres1_r = [res("xres0"), res("xres1")]
        xres1_d = [new_dsem(), new_dsem()]
        xtmp_r = [res("xtmp0"), res("xtmp1")]
        xtmp_d = [new_dsem(), new_dsem()]
        rope_r = [res("rope0"), res("rope1")]
        rope_d = [new_dsem(), new_dsem()]
        tab_d = new_dsem()
        tab_d2 = new_dsem()
        kn_d = new_dsem()
        vout_d = new_dsem()
        misc2_d = new_dsem()
        misc3_d = new_dsem()

        P.dma("sync", qg_t, qg.to_broadcast([128, 128]), misc2_d, writes=[res("qg")])
        P.dma("sync", kg_t, kg.to_broadcast([128, 128]), misc2_d, writes=[res("kg")])
        P.dma("sync", tab_g, ln1_g.to_broadcast([128, D]), tab_d, writes=[res("tab_g")])
        P.dma("sync", tab_b, ln1_b.to_broadcast([128, D]), tab_d2, writes=[res("tab_b")])
        P.dma("pool", bands_sb, bands.rearrange("p (m n) -> p m n", n=128), misc3_d, writes=[res("bands")])
        P.dma("pool", wp, w_pool.rearrange("(a p) e -> p a e", p=128), misc3_d, writes=[res("wp")])
        for _n in ("qg", "kg"):
            res(_n).w = {misc2_d.sem: (misc2_d.cnt, False)}
        for _n in ("bands", "wp"):
            res(_n).w = {misc3_d.sem: (misc3_d.cnt, False)}

        xhat_r = [res("xhat0"), res("xhat1")]
        xh_i = [0]

        def ln_stats(x_ap, x_res):
            i = mv_i[0] % 4
            mv_i[0] += 1
            mv, st = mvs[i], sts[i]
            r = res("mv%d" % i)
            for c in range(4):
                P.op("dve", lambda e, c=c: e.bn_stats(out=st[:, c * 6:(c + 1) * 6], in_=x_ap[:, c * 512:(c + 1) * 512]),
                     reads=[x_res], swrites=[r])
            P.op("dve", lambda e: e.bn_aggr(out=mv[:, 0:2], in_=st), reads=[r], swrites=[r])
            P.op("act", lambda e: e.activation(out=mv[:, 2:3], in_=mv[:, 1:2], func=AF.Sqrt, bias=eps_t[:, 0:1], scale=1.0),
                 reads=[r, res("eps_t")], swrites=[r])
            P.op("dve", lambda e: e.reciprocal(out=mv[:, 2:3], in_=mv[:, 2:3]), reads=[r], writes=[r])
            P.op("dve", lambda e: e.tensor_scalar(out=mv[:, 3:4], in0=mv[:, 0:1], scalar1=-1.0, scalar2=mv[:, 2:3],
                                                  op0=ALU.mult, op1=ALU.mult), reads=[r], swrites=[r])
            return mv, r

        class FrontItem:
            def __init__(self, x_ap, x_res, setc, sc_base, sh_base, dst_fn, uT_r, load=None):
                self.x_ap, self.x_res, self.setc = x_ap, x_res, setc
                self.sc_base, self.sh_base = sc_base, sh_base
                self.dst_fn, self.uT_r = dst_fn, uT_r
                self.load = load

        tr_i = [0]

        def fr_load(it):
            if it.load is not None:
                it.load()
                it.load = None

        def fr_A(it):
            fr_load(it)
            it.xi = xh_i[0] % 2
            xh_i[0] += 1
            mv, r = ln_stats(it.x_ap, it.x_res)
            xh = xhats[it.xi]
            P.op("act", lambda e: e.activation(out=xh, in_=it.x_ap, func=AF.Identity, bias=mv[:, 3:4], scale=mv[:, 2:3]),
                 reads=[it.x_res, r], writes=[xhat_r[it.xi]])

        def fr_B(it):
            it.tb = 2 * (tr_i[0] % 2)
            tr_i[0] += 1
            xh = xhats[it.xi]
            for kc in range(16):
                b = it.tb + kc // 8
                P.op("pe", lambda e, kc=kc, b=b: e.transpose(
                    out=PSBb[b][:, (kc % 8) * 128:(kc % 8 + 1) * 128],
                    in_=xh[:, kc * 128:(kc + 1) * 128], identity=ident_b),
                    reads=[xhat_r[it.xi], res("ident_b")], writes=[PSR[b]], inc=(kc % 8 == 7))

        def fr_C(it):
            for kc in range(16):
                b = it.tb + kc // 8
                src = PSBb[b][:, (kc % 8) * 128:(kc % 8 + 1) * 128]
                dst = it.dst_fn(kc)
                sc = modT[:, it.sc_base + kc, it.setc:it.setc + 1]
                sh = modT[:, it.sh_base + kc, it.setc:it.setc + 1]
                if kc < 8:
                    P.op("act", lambda e, src=src, dst=dst, sc=sc, sh=sh: e.activation(
                        out=dst, in_=src, func=AF.Identity, bias=sh, scale=sc),
                        reads=[PSR[b], res("modTa" if it.sc_base < 32 else "modTc")], swrites=[it.uT_r])
                else:
                    P.op("dve", lambda e, src=src, dst=dst, sc=sc, sh=sh: e.tensor_scalar(
                        out=dst, in0=src, scalar1=sc, scalar2=sh, op0=ALU.mult, op1=ALU.add),
                        reads=[PSR[b], res("modTa" if it.sc_base < 32 else "modTc")], swrites=[it.uT_r])

        def proj_tok(slot, sv, sr, bank):
            uT_r = r_uT[slot]
            srs = [sr] if sr is not None else [xres1_r[0], xres1_r[1]]
            for kc in range(16):
                P.op("pe", lambda e, kc=kc: e.matmul(PSB[bank], lhsT=uT[:, slot, kc, :],
                                                     rhs=sv[:, kc, :], start=(kc == 0), stop=(kc == 15)),
                     reads=[uT_r] + srs, writes=[PSR[bank]], inc=(kc == 15))

        ss_i = [0]
        qr_i = [0]

        class QK:
            pass

        def qk_E(u):
            nh, bank = u.nh, u.bank
            W = nh * 128
            i = ss_i[0] % 4
            ss_i[0] += 1
            ss = ssb[i]
            rs = res("ss%d" % i)
            for h in range(nh):
                P.op("act", lambda e, h=h: e.activation(out=t1[:, h * 128:(h + 1) * 128],
                                                        in_=PSB[bank][:, h * 128:(h + 1) * 128],
                                                        func=AF.Square, accum_out=ss[:, h:h + 1]),
                     reads=[PSR[bank]], swrites=[rs, res("t1")])
            P.op("act", lambda e: e.activation(out=ss[:, 4:4 + nh], in_=ss[:, 0:nh], func=AF.Sqrt, bias=eps_t[:, 0:1],
                                               scale=1.0 / 128), reads=[rs, res("eps_t")], swrites=[rs])
            P.op("dve", lambda e: e.reciprocal(out=ss[:, 4:4 + nh], in_=ss[:, 4:4 + nh]), reads=[rs], writes=[rs])
            for h in range(nh):
                P.op("dve", lambda e, h=h: e.scalar_tensor_tensor(
                    out=kn[:, h * 128:(h + 1) * 128], in0=PSB[bank][:, h * 128:(h + 1) * 128],
                    scalar=ss[:, 4 + h:5 + h], in1=u.gain_t, op0=ALU.mult, op1=ALU.mult),
                    reads=[PSR[bank], rs, u.gain_r], swrites=[res("kn")])
            if u.kout is not None:
                u.kout()
            u.qi = qr_i[0] % 2
            qr_i[0] += 1
            qr = qrs[u.qi]
            qr_r = res("qr%d" % u.qi)
            if u.rope is not None:
                rope_t, rope_rr = u.rope()
                C = rope_t[:, 0:128]
                S = rope_t[:, 128:256]
                kn3 = kn[:, 0:W].rearrange("p (h n) -> p h n", n=128)
                t13 = t1[:, 0:W].rearrange("p (h n) -> p h n", n=128)
                P.op("dve", lambda e: e.tensor_tensor(out=t13, in0=kn3, in1=C.unsqueeze(1).to_broadcast([128, nh, 128]),
                                                       op=ALU.mult), reads=[res("kn"), rope_rr], writes=[res("t1")])
                kn5 = kn[:, 0:W].rearrange("p (h a b c) -> p h a b c", a=2, b=2, c=32)
                t25 = t2[:, 0:W].rearrange("p (h a b c) -> p h a b c", a=2, b=2, c=32)
                S4 = S.rearrange("p (a b c) -> p a b c", a=2, b=2, c=32)
                for bsel in range(2):
                    P.op("dve", lambda e, bsel=bsel: e.tensor_tensor(
                        out=t25[:, :, :, bsel, :], in0=kn5[:, :, :, 1 - bsel, :],
                        in1=S4[:, :, bsel, :].unsqueeze(1).to_broadcast([128, nh, 2, 32]), op=ALU.mult),
                        reads=[res("kn"), rope_rr], swrites=[res("t2")])
                P.op("dve", lambda e: e.tensor_tensor(out=qr[:, 0:W], in0=t1[:, 0:W], in1=t2[:, 0:W], op=ALU.add),
                     reads=[res("t1"), res("t2")], writes=[qr_r])
            else:
                P.op("dve", lambda e: e.tensor_copy(out=qr[:, 0:W], in_=kn[:, 0:W]), reads=[res("kn")], writes=[qr_r])

        def qk_F(u):
            nh = u.nh
            W = nh * 128
            qr = qrs[u.qi]
            qr_r = res("qr%d" % u.qi)
            tbank = u.tbank
            for h in range(nh):
                P.op("pe", lambda e, h=h: e.transpose(out=PSBb[tbank][:, h * 128:(h + 1) * 128],
                                                      in_=qr[:, h * 128:(h + 1) * 128], identity=ident_b),
                     reads=[qr_r, res("ident_b")], writes=[PSR[tbank]], inc=(h == nh - 1))
            P.op("act", lambda e: u.dst_fn(e, PSBb[tbank][:, 0:W]), reads=[PSR[tbank]], swrites=[u.dst_r])

        r_KT, r_V = res("KT"), res("V")
        r_uT = [res("uT%d" % i) for i in range(4)]
        r_QT, r_mixO, r_mixP = res("QT"), res("mixO"), res("mixP")
        r_pTok, r_pooled = res("pTok"), res("pooledT")

        def load_rope(ltile):
            i = ltile % 2
            P.dma("sync", rope_b[i], rope[ltile * 128:(ltile + 1) * 128, :], rope_d[i], writes=[rope_r[i]])
            return rope_b[i], rope_r[i]

        pj_i = [0]

        def lazy_slab(batch, idx):
            box = {}

            def get():
                if "v" not in box:
                    box["v"], box["r"] = slab_from_scratch(batch, idx)
                return box["v"], box["r"]
            return get

        def make_kv_unit(slot, key_off, chunk, rope_tile, slab_get, out_rows=None):
            u = QK()
            u.nh, u.gain_t, u.gain_r = 2, kg_t, res("kg")
            u.rope = (lambda: load_rope(rope_tile)) if rope_tile is not None else None
            u.tbank = 6
            u.dst_r = r_KT

            def dst_fn(e, src):
                return e.copy(out=KT[:, :, key_off:key_off + 128], in_=src.rearrange("p (h n) -> p h n", n=128))
            u.dst_fn = dst_fn

            def D_stage():
                u.bank = 4 + (pj_i[0] % 2)
                pj_i[0] += 1
                sv, sr = slab_get()
                proj_tok(slot, sv, sr, u.bank)
                P.op("act", lambda e: e.copy(out=Vb[:, chunk, :], in_=PSB[u.bank][:, 256:512]),
                     reads=[PSR[u.bank]], swrites=[r_V])
            u.D = D_stage
            u.kout = None
            if out_rows is not None:
                def kout():
                    P.dma("sync", cko[out_rows:out_rows + 128, :], kn[:, 0:256], kn_d, reads=[res("kn")])
                    P.op("act", lambda e: e.copy(out=vout, in_=PSB[u.bank][:, 256:512]), reads=[PSR[u.bank]], writes=[res("vout")])
                    P.dma("sync", cvo[out_rows:out_rows + 128, :], vout, vout_d, reads=[res("vout")])
                u.kout = kout
            return u

        kvslab_d = new_dsem()
        P.dma("pool", kvslab, wslab(w_in, 1024), kvslab_d, writes=[xres1_r[0], xres1_r[1]])
        convA = [(sq, wslab(w_in, sq * 512)) for sq in range(2)]
        convA += [(2 + sp, wslab(w_in, 1536 + sp * 512)) for sp in range(2)]
        convA += [(4, wslab(w_in, 1024))]
        convA += [(5 + ds, wslab(w_out, ds * 512)) for ds in range(4)]
        convB = [(N_SLAB_A + s, wslab(w_ff1, s * 512)) for s in range(16)]
        convB += [(N_SLAB_A + 16 + ds * 4 + kq, wslab(w_ff2, ds * 512, kq * 16)) for ds in range(4) for kq in range(4)]

        class KVItem:
            pass

        uTst_d = [new_dsem() for _ in range(4)]
        uTld_d = [new_dsem() for _ in range(4)]
        r_uTs = res("uTs")

        def uts_rows(lt):
            idx = lt if lt <= 16 else 17
            return uTs[idx * 128:(idx + 1) * 128, :].rearrange("p (k n) -> p k n", n=128)

        def store_uT(lt):
            if lt <= 16 or lt == 31:
                sl = lt % 4
                P.dma("sync", uts_rows(lt), uT[:, sl, :, :], uTst_d[sl], reads=[r_uT[sl]])

        kv_items = []
        for lt in range(32):
            it = KVItem()
            i = lt % 2
            it.fr = FrontItem(xtmp[i], xtmp_r[i], 0, 16, 0, (lambda kc, sl=lt % 4: uT[:, sl, kc, :]), r_uT[lt % 4],
                              load=(lambda lt=lt, i=i: P.dma("sync", xtmp[i], xs[lt * 128:(lt + 1) * 128, :], xtmp_d[i],
                                                            writes=[xtmp_r[i]])))
            it.u = make_kv_unit(lt % 4, lt * 128, lt, lt, (lambda: (kvslab, None)))
            it.lt = lt
            it.mod_s = (8 + lt // 2) if (lt % 2 == 1 and lt < 12) else None
            it.conv = convA[lt // 2] if (lt % 2 == 0 and lt // 2 < len(convA)) else None
            kv_items.append(it)
        pipeline(kv_items, [lambda it: (fr_A(it.fr), (convert(0, *it.conv) if it.conv is not None else None)),
                            lambda it: fr_B(it.fr), lambda it: fr_C(it.fr),
                            lambda it: (it.u.D(), store_uT(it.lt)), lambda it: qk_E(it.u), lambda it: qk_F(it.u),
                            lambda it: (mod_slabs(it.mod_s, it.mod_s + 1, fixed_bank=7) if it.mod_s is not None else None)])
        conv_done(0)
        r_uTs.w = {d.sem: (d.cnt, False) for d in uTst_d if d.cnt}
        ckv = xtmp[0].rearrange("p (t c) -> p t c", c=256)[:, 0:4, :]
        cvv = xtmp[1].rearrange("p (t c) -> p t c", c=256)[:, 0:4, :]
        P.dma("sync", ckv, ck.rearrange("(t p) c -> p t c", p=128), xtmp_d[0], writes=[xtmp_r[0]])
        P.dma("sync", cvv, cv.rearrange("(t p) c -> p t c", p=128), xtmp_d[1], writes=[xtmp_r[1]])
        P.op("dve", lambda e: e.tensor_copy(out=xhats[0][:, 0:1024], in_=xtmp[0][:, 0:1024]), reads=[xtmp_r[0]], writes=[xhat_r[0]])
        P.op("dve", lambda e: e.tensor_copy(out=Vb[:, 32:36, :], in_=cvv), reads=[xtmp_r[1]], swrites=[r_V])
        for t in range(4):
            for h in range(2):
                P.op("pe", lambda e, t=t, h=h: e.transpose(
                    out=PSBb[6][:, (t * 2 + h) * 128:(t * 2 + h + 1) * 128],
                    in_=xhats[0][:, t * 256 + h * 128:t * 256 + (h + 1) * 128], identity=ident_b),
                    reads=[xhat_r[0], res("ident_b")], writes=[PSR[6]], inc=(t == 3 and h == 1))
        P.op("act", lambda e: e.copy(out=KT[:, :, 4096:4608].rearrange("p h (t n) -> p t h n", n=128),
                                     in_=PSBb[6][:, 0:1024].rearrange("p (t h n) -> p t h n", h=2, n=128)),
             reads=[PSR[6]], swrites=[r_KT])


        BK = {"B0_mid": 0, "Bm1_std": 1, "Bp1_std": 2, "B0_fo": 3, "Bm1_fo": 4, "B0_lo": 5, "Bp1_lo": 6,
              "B0_pf": 7, "B0_pl": 8}

        def band(kind, w):
            return bands_sb[:, BK[kind] * 4 + w, :]

        def ln_affine_steps(x_ap, x_res, dst, dsem):
            st = {}

            def s0():
                i = mv_i[0] % 4
                mv_i[0] += 1
                st["mv"], st["st"], st["r"] = mvs[i], sts[i], res("mv%d" % i)
                mv, stt, r = st["mv"], st["st"], st["r"]
                for c in range(4):
                    P.op("dve", lambda e, c=c: e.bn_stats(out=stt[:, c * 6:(c + 1) * 6], in_=x_ap[:, c * 512:(c + 1) * 512]),
                         reads=[x_res], swrites=[r])
                P.op("dve", lambda e: e.bn_aggr(out=mv[:, 0:2], in_=stt), reads=[r], swrites=[r])

            def s1():
                mv, r = st["mv"], st["r"]
                P.op("act", lambda e: e.activation(out=mv[:, 2:3], in_=mv[:, 1:2], func=AF.Sqrt, bias=eps_t[:, 0:1], scale=1.0),
                     reads=[r, res("eps_t")], swrites=[r])

            def s2():
                mv, r = st["mv"], st["r"]
                P.op("dve", lambda e: e.reciprocal(out=mv[:, 2:3], in_=mv[:, 2:3]), reads=[r], writes=[r])
                P.op("dve", lambda e: e.tensor_scalar(out=mv[:, 3:4], in0=mv[:, 0:1], scalar1=-1.0, scalar2=mv[:, 2:3],
                                                      op0=ALU.mult, op1=ALU.mult), reads=[r], swrites=[r])

            def s3():
                mv, r = st["mv"], st["r"]
                P.op("act", lambda e: e.activation(out=x_ap, in_=x_ap, func=AF.Identity, bias=mv[:, 3:4], scale=mv[:, 2:3]),
                     reads=[x_res, r], writes=[x_res])

            def s4():
                P.op("dve", lambda e: e.tensor_tensor(out=x_ap, in0=x_ap, in1=tab_g, op=ALU.mult),
                     reads=[x_res, res("tab_g")], writes=[x_res])

            def s5():
                P.op("dve", lambda e: e.tensor_tensor(out=x_ap, in0=x_ap, in1=tab_b, op=ALU.add),
                     reads=[x_res, res("tab_b")], writes=[x_res])

            def s6():
                P.dma("sync", dst, x_ap, dsem, reads=[x_res])
            return [s0, s1, s2, s3, s4, s5, s6]

        def ln_affine_store(x_ap, x_res, dst, dsem):
            mv, r = ln_stats(x_ap, x_res)
            P.op("act", lambda e: e.activation(out=x_ap, in_=x_ap, func=AF.Identity, bias=mv[:, 3:4], scale=mv[:, 2:3]),
                 reads=[x_res, r], writes=[x_res])
            P.op("dve", lambda e: e.tensor_tensor(out=x_ap, in0=x_ap, in1=tab_g, op=ALU.mult),
                 reads=[x_res, res("tab_g")], writes=[x_res])
            P.op("dve", lambda e: e.tensor_tensor(out=x_ap, in0=x_ap, in1=tab_b, op=ALU.add),
                 reads=[x_res, res("tab_b")], writes=[x_res])
            P.dma("sync", dst, x_ap, dsem, reads=[x_res])

        def make_fronts(spec):
            x_dram, own_rows, halo_rows, setc = spec["x"], spec["own_rows"], spec["halo_rows"], spec["setc"]
            fitems = {}
            for j in range(2):
                fitems[1 + j] = FrontItem(
                    xres1[j], xres1_r[j], setc, 16, 0, (lambda kc, sl=1 + j: uT[:, sl, kc, :]), r_uT[1 + j],
                    load=(lambda j=j: P.dma("sync", xres1[j], x_dram[own_rows[j]:own_rows[j] + 128, :], xres1_d[j],
                                            writes=[xres1_r[j]])))
            if halo_rows is not None:
                for j, sl in ((0, 0), (1, 3)):
                    fitems[sl] = FrontItem(
                        xtmp[j], xtmp_r[j], setc, 16, 0, (lambda kc, sl=sl: uT[:, sl, kc, :]), r_uT[sl],
                        load=(lambda j=j: P.dma("sync", xtmp[j], x_dram[halo_rows[j]:halo_rows[j] + 128, :], xtmp_d[j],
                                                writes=[xtmp_r[j]])))
            return fitems

        def prefetch(spec):
            for sl, lt in enumerate(spec["ut_tiles"]):
                P.dma("sync", uT[:, sl, :, :], uts_rows(lt), uTld_d[sl], reads=[r_uTs], writes=[r_uT[sl]])
            for j in range(2):
                ap, rr, dd = spec["ob"][j]
                r0 = spec["own_rows"][j]
                P.dma("sync", ap, spec["x"][r0:r0 + 128, :], dd, writes=[rr])

        def group1(spec, fitems=None, next_spec=None):
            setc, rope_tiles, n_kc, band_cfg = spec["setc"], spec["rope_tiles"], spec["n_kc"], spec["band_cfg"]
            x1_row0, own_kv = spec["x1_row0"], spec["own_kv"]
            ob = spec["ob"]
            if fitems is None and not spec["sample"]:
                fitems = make_fronts(spec)
                order = [1, 2] + [sl for sl in (0, 3) if sl in fitems]
                pipeline([fitems[sl] for sl in order], [fr_A, fr_B, fr_C])
            slots = spec["pproj_slots"]
            for sp in range(2):
                pre = spec.get("pre", {}).get(sp)
                sv, sr = pre if pre is not None else slab_from_scratch(0, 2 + sp)
                for sl in slots:
                    bank = 4 + (pj_i[0] % 2)
                    pj_i[0] += 1
                    proj_tok(sl, sv, sr, bank)
                    P.op("act", lambda e, sl=sl, sp=sp, bank=bank: e.copy(out=pTok[:, sl, sp * 512:(sp + 1) * 512], in_=PSB[bank]),
                         reads=[PSR[bank]], swrites=[r_pTok])
            units = []
            if own_kv is not None:
                sg = lazy_slab(0, 4)
                for j in range(2):
                    units.append(make_kv_unit(1 + j, j * 128, j, None, sg, out_rows=own_kv + j * 128))
            for sq in range(2):
                sg = lazy_slab(0, sq)
                for j in range(2):
                    u = QK()
                    u.nh, u.gain_t, u.gain_r = 4, qg_t, res("qg")
                    u.rope = (lambda j=j: load_rope(rope_tiles[j])) if rope_tiles is not None else None
                    u.tbank = 6
                    u.dst_r = r_QT
                    u.kout = None

                    def dst_fn(e, src, sq=sq, j=j):
                        return e.copy(out=QT.rearrange("p (h n) -> p h n", n=256)[:, 4 * sq:4 * sq + 4, j * 128:(j + 1) * 128],
                                      in_=src.rearrange("p (h n) -> p h n", n=128))
                    u.dst_fn = dst_fn

                    def D_stage(u=u, j=j, sg=sg):
                        u.bank = 4 + (pj_i[0] % 2)
                        pj_i[0] += 1
                        sv, sr = sg()
                        proj_tok(1 + j, sv, sr, u.bank)
                    u.D = D_stage
                    units.append(u)
            pipeline(units, [lambda u: u.D(), qk_E, qk_F])
            for j in range(2):
                sl = 1 + j
                for half in range(2):
                    bank = 0 + 2 * j + half
                    for q4 in range(4):
                        a = half * 4 + q4
                        w, cc = a // 2, a % 2
                        cfg = band_cfg[sl]
                        for n, (src_sl, kind) in enumerate(cfg):
                            P.op("pe", lambda e, q4=q4, w=w, cc=cc, src_sl=src_sl, kind=kind, n=n, bank=bank, cfg=cfg: e.matmul(
                                PSB[bank][:, q4 * 128:(q4 + 1) * 128],
                                lhsT=pTok[:, src_sl, w * 256 + cc * 128:w * 256 + (cc + 1) * 128],
                                rhs=band(kind, w), start=(n == 0), stop=(n == len(cfg) - 1)),
                                reads=[r_pTok, res("bands")], writes=[PSR[bank]], inc=(q4 == 3 and n == len(cfg) - 1))
                    P.op("act", lambda e, half=half, j=j, bank=bank: e.copy(
                        out=pooledT[:, half * 4:half * 4 + 4, j * 128:(j + 1) * 128],
                        in_=PSB[bank].rearrange("p (a n) -> p a n", n=128)),
                        reads=[PSR[bank]], swrites=[r_pooled])
            for w in range(4):
                bank = 4 + (w % 2)
                for ec in range(2):
                    for cc in range(2):
                        P.op("pe", lambda e, w=w, ec=ec, cc=cc, bank=bank: e.matmul(
                            PSB[bank][:, ec * 256:(ec + 1) * 256], lhsT=wp[:, w * 2 + cc, ec * 128:(ec + 1) * 128],
                            rhs=pooledT[:, w * 2 + cc, :], start=(cc == 0), stop=(cc == 1)),
                            reads=[res("wp"), r_pooled], writes=[PSR[bank]], inc=(ec == 1 and cc == 1))
                for ec in range(2):
                    P.op("act", lambda e, w=w, ec=ec, bank=bank: e.activation(
                        out=mixT[:, 8 + w * 2 + ec, :], in_=PSB[bank][:, ec * 256:(ec + 1) * 256],
                        func=AF.Identity, bias=0.0, scale=psT[:, w * 2 + ec:w * 2 + ec + 1]),
                        reads=[PSR[bank], res("psT")], swrites=[r_mixP])
            mixf = mixT.rearrange("p k n -> p (k n)")

            def S_mm(h2, kc):
                sb = kc % 2
                for jj in range(2):
                    P.op("pe", lambda e, kc=kc, sb=sb, jj=jj, h2=h2: e.matmul(
                        PSB[2 * sb + jj], lhsT=KT[:, h2, kc * 128:(kc + 1) * 128],
                        rhs=QT[:, (4 * h2 + 2 * jj) * 256:(4 * h2 + 2 * jj + 2) * 256], start=True, stop=True),
                        reads=[r_KT, r_QT], writes=[PSR[2 * sb + jj]], inc=(jj == 1))

            mod_pre = None
            if spec.get("mod_s") is not None:
                mod_pre = slab_load(wslab(w_mod, spec["mod_s"] * 512))
            seq = [(h2, kc) for h2 in range(2) for kc in range(n_kc)]

            def S_i(i):
                h2, kc = seq[i]
                sb = i % 2
                for jj in range(2):
                    P.op("pe", lambda e, kc=kc, sb=sb, jj=jj, h2=h2: e.matmul(
                        PSB[2 * sb + jj], lhsT=KT[:, h2, kc * 128:(kc + 1) * 128],
                        rhs=QT[:, (4 * h2 + 2 * jj) * 256:(4 * h2 + 2 * jj + 2) * 256], start=True, stop=True),
                        reads=[r_KT, r_QT], writes=[PSR[2 * sb + jj]], inc=(jj == 1))

            S_i(0)
            S_i(1)
            for i, (h2, kc) in enumerate(seq):
                sb = i % 2
                pb = i % 3
                P.op("act", lambda e, sb=sb, pb=pb: e.activation(out=PTe[pb], in_=ps2(2 * sb), func=AF.Exp,
                                                                 bias=eps_t[:, 1:2], scale=ATT_SCALE),
                     reads=[PSR[2 * sb], PSR[2 * sb + 1], res("eps_t")], writes=[res("PTe%d" % pb)])
                if i + 2 < len(seq):
                    S_i(i + 2)
                for jj in range(2):
                    P.op("pe", lambda e, h2=h2, kc=kc, pb=pb, jj=jj: e.matmul(
                        PSB[4 + jj], lhsT=Vb[:, kc, h2 * 128:(h2 + 1) * 128],
                        rhs=PTe[pb][:, jj * 512:(jj + 1) * 512], start=(kc == 0), stop=(kc == n_kc - 1)),
                        reads=[r_V, res("PTe%d" % pb)], writes=[PSR[4 + jj]], inc=False)
                for jj in range(2):
                    P.op("pe", lambda e, kc=kc, pb=pb, jj=jj: e.matmul(
                        PSB[6 + jj], lhsT=ones_b, rhs=PTe[pb][:, jj * 512:(jj + 1) * 512],
                        start=(kc == 0), stop=(kc == n_kc - 1)),
                        reads=[res("ones_b"), res("PTe%d" % pb)], writes=[PSR[6 + jj]], inc=(jj == 1))
                if kc == n_kc - 1:
                    P.op("dve", lambda e: e.reciprocal(out=acc, in_=ps2(6)), reads=[PSR[6], PSR[7]], writes=[res("acc")])
                    for jj in range(2):
                        c0 = (4 * h2 + 2 * jj) * 256
                        P.op("dve", lambda e, jj=jj, c0=c0: e.tensor_tensor(
                            out=mixf[:, c0:c0 + 512], in0=PSB[4 + jj], in1=acc[:, jj * 512:(jj + 1) * 512], op=ALU.mult),
                            reads=[PSR[4 + jj], res("acc")], swrites=[r_mixO])
            if mod_pre is not None:
                mod_slabs(spec["mod_s"], spec["mod_s"] + 1, fixed_bank=7, preloaded=mod_pre)
            nfit = None
            if next_spec is not None and next_spec["sample"]:
                prefetch(next_spec)
            elif next_spec is not None:
                nfit = make_fronts(next_spec)
                halos = [nfit[sl] for sl in (0, 3) if sl in nfit]
                if halos:
                    pipeline(halos, [fr_A, fr_B, fr_C])
            for ds in range(4):
                sv, sr = slab_from_scratch(0, 5 + ds)
                for dc in range(4):
                    bank = 4 + (dc % 2)
                    for kc in range(16):
                        P.op("pe", lambda e, dc=dc, kc=kc, sv=sv, bank=bank: e.matmul(
                            PSB[bank][:, 0:256], lhsT=sv[:, kc, dc * 128:(dc + 1) * 128], rhs=mixT[:, kc, :],
                            start=(kc == 0), stop=(kc == 15)),
                            reads=[sr, r_mixO, r_mixP], writes=[PSR[bank]], inc=(kc == 15))
                    P.op("act", lambda e, dc=dc, ds=ds, bank=bank: e.activation(
                        out=fT1[:, dc, :], in_=PSB[bank][:, 0:256], func=AF.Identity, bias=0.0,
                        scale=modT[:, 32 + ds * 4 + dc, setc:setc + 1]),
                        reads=[PSR[bank], res("modTb")], swrites=[res("fT1")])
                for j in range(2):
                    bank = 2 * (ds % 2) + j
                    for dc in range(4):
                        P.op("pe", lambda e, dc=dc, j=j, bank=bank: e.transpose(
                            out=PSB[bank][:, dc * 128:(dc + 1) * 128], in_=fT1[:, dc, j * 128:(j + 1) * 128],
                            identity=ident_f), reads=[res("fT1"), res("ident_f")], writes=[PSR[bank]], inc=(dc == 3))
                    P.op("dve", lambda e, j=j, ds=ds, bank=bank: e.scalar_tensor_tensor(
                        out=ob[j][0][:, ds * 512:(ds + 1) * 512], in0=ob[j][0][:, ds * 512:(ds + 1) * 512],
                        scalar=ALPHA, in1=PSB[bank], op0=ALU.mult, op1=ALU.add),
                        reads=[PSR[bank], ob[j][1]], writes=[ob[j][1]])
            if next_spec is not None and next_spec["sample"]:
                next_spec["pre"] = {sp: slab_from_scratch(0, 2 + sp) for sp in range(2)}
            def ln_stage(j):
                ln_affine_store(ob[j][0], ob[j][1], x1s[x1_row0 + j * 128:x1_row0 + (j + 1) * 128, :], ob[j][2])
            if nfit is None:
                for j in range(2):
                    ln_stage(j)
            else:
                pipeline([0, 1], [ln_stage, lambda j: fr_load(nfit[1 + j]), lambda j: fr_A(nfit[1 + j]),
                                  lambda j: fr_B(nfit[1 + j]), lambda j: fr_C(nfit[1 + j])])
            return nfit

        OB = [[(xres1[j], xres1_r[j], xres1_d[j]) for j in range(2)],
              [(xtmp[j], xtmp_r[j], xtmp_d[j]) for j in range(2)]]
        specs = []
        for g in range(8):
            t0 = 2 * g
            cfg = {
                1: [(0, "Bm1_fo" if g == 0 else "Bm1_std"), (1, "B0_fo" if g == 0 else "B0_mid"), (2, "Bp1_std")],
                2: [(1, "Bm1_std"), (2, "B0_lo" if g == 7 else "B0_mid"), (3, "Bp1_lo" if g == 7 else "Bp1_std")],
            }
            specs.append(dict(x=xs, own_rows=[t0 * 128, (t0 + 1) * 128],
                              halo_rows=[((t0 - 1) % 32) * 128, (t0 + 2) * 128], setc=0, rope_tiles=[t0, t0 + 1],
                              n_kc=36, band_cfg=cfg, x1_row0=g * 256, own_kv=None, mod_s=14 + g,
                              sample=True, ob=OB[g % 2], pproj_slots=[0, 1, 2, 3],
                              ut_tiles=[(t0 - 1) % 32, t0, t0 + 1, t0 + 2]))
        for sq_ in range(2):
            cfg = {1: [(1, "B0_pf"), (2, "Bp1_std")], 2: [(1, "Bm1_std"), (2, "B0_pl")]}
            specs.append(dict(x=xp, own_rows=[sq_ * 256, sq_ * 256 + 128], halo_rows=None, setc=1, rope_tiles=None,
                              n_kc=2, band_cfg=cfg, x1_row0=2048 + sq_ * 256, own_kv=sq_ * 256, mod_s=22 + sq_,
                              sample=False, ob=OB[0], pproj_slots=[1, 2]))
        prefetch(specs[0])
        fit = None
        for g, spec in enumerate(specs):
            if g < 8:
                for cidx, csrc in convB[4 * g:4 * g + 4]:
                    convert(1, cidx, csrc)
            if g == 8:
                conv_done(1)
            fit = group1(spec, fit, specs[g + 1] if g + 1 < len(specs) else None)
        mod_plus1(64)

        P.barrier()
        c2 = Carver(big, pers_end)
        slab3 = c2.bf16(16 * 512)
        xres2 = [c2.f32(D) for _ in range(4)]
        uT2 = c2.bf16(16 * 512).rearrange("p (k n) -> p k n", n=512)
        hT = c2.bf16(64 * 512).rearrange("p (k n) -> p k n", n=512)
        fT2 = c2.f32(4 * 512).rearrange("p (c n) -> p c n", n=512)
        rtmp = [c2.f32(512) for _ in range(2)]
        spare = c2.f32(D)
        print('c2.off', c2.off)
        assert c2.off <= NW, c2.off
        slab_bufs.append(slab3)
        slab_state["n"] = 3
        xres2_r = [res("x2_%d" % t) for t in range(4)]
        xres2_d = [xres1_d[0], xres1_d[1], xtmp_d[0], xtmp_d[1]]
        r_uT2 = [res("uT2_%d" % i) for i in range(4)]
        r_hT, r_fT2 = res("hT"), res("fT2")
        rt_r = [res("rtmp0"), res("rtmp1")]

        P.dma("sync", tab_g, ln2_g.to_broadcast([128, D]), tab_d, writes=[res("tab_g")])
        P.dma("sync", tab_b, ln2_b.to_broadcast([128, D]), tab_d2, writes=[res("tab_b")])

        spare_r, spare_d = res("spare"), new_dsem()

        def front2_item(g, t):
            setc = 0 if g < 4 else 1
            row = g * 512 + t * 128
            return FrontItem(spare, spare_r, setc, 64, 48, (lambda kc, t=t: uT2[:, kc, t * 128:(t + 1) * 128]), r_uT2[t],
                             load=(lambda row=row: P.dma("sync", spare, x1s[row:row + 128, :], spare_d, writes=[spare_r])))

        def load_x1(g, t):
            row = g * 512 + t * 128
            P.dma("sync", xres2[t], x1s[row:row + 128, :], xres2_d[t], writes=[xres2_r[t]])

        def ln2_stage(g, t):
            row = g * 512 + t * 128
            dst = ys[row:row + 128, :] if g < 4 else yp[t * 128:(t + 1) * 128, :]
            ln_affine_store(xres2[t], xres2_r[t], dst, xres2_d[t])

        def ln2_steps(g, t):
            row = g * 512 + t * 128
            dst = ys[row:row + 128, :] if g < 4 else yp[t * 128:(t + 1) * 128, :]
            return ln_affine_steps(xres2[t], xres2_r[t], dst, xres2_d[t]) + [lambda: load_x1(g + 1, t)]

        for t in range(4):
            load_x1(0, t)
        pipeline([front2_item(0, t) for t in range(4)], [fr_A, fr_B, fr_C])
        for g in range(5):
            setc = 0 if g < 4 else 1
            n_ev = 0
            pend = [ln2_steps(g - 1, t) for t in range(4)] if g > 0 else None
            for s in range(16):
                sv, sr = slab_from_scratch(1, N_SLAB_A + s)
                for j in range(4):
                    bank = 4 + ((s * 4 + j) % 4)
                    for kc in range(16):
                        P.op("pe", lambda e, j=j, kc=kc, sv=sv, bank=bank: e.matmul(
                            PSB[bank], lhsT=sv[:, kc, j * 128:(j + 1) * 128], rhs=uT2[:, kc, :],
                            start=(kc == 0), stop=(kc == 15)),
                            reads=[sr] + r_uT2, writes=[PSR[bank]], inc=(kc == 15))
                    ri = n_ev % 2
                    n_ev += 1
                    P.op("act", lambda e, bank=bank, ri=ri: e.activation(out=rtmp[ri], in_=PSB[bank], func=AF.Relu),
                         reads=[PSR[bank]], writes=[rt_r[ri]])
                    P.op("dve", lambda e, s=s, j=j, ri=ri: e.tensor_tensor(out=hT[:, s * 4 + j, :], in0=rtmp[ri], in1=rtmp[ri],
                                                                         op=ALU.mult),
                         reads=[rt_r[ri]], swrites=[r_hT])
                if g > 0:
                    for t in range(4):
                        k = s - t
                        if 0 <= k < len(pend[t]):
                            pend[t][k]()
            nfi = [front2_item(g + 1, t) for t in range(4)] if g < 4 else None
            for ds in range(4):
                for kq in range(4):
                    sv, sr = slab_from_scratch(1, N_SLAB_A + 16 + ds * 4 + kq)
                    for dc in range(4):
                        for kc in range(16):
                            P.op("pe", lambda e, dc=dc, kc=kc, kq=kq, sv=sv: e.matmul(
                                PSB[4 + dc], lhsT=sv[:, kc, dc * 128:(dc + 1) * 128], rhs=hT[:, kq * 16 + kc, :],
                                start=(kq == 0 and kc == 0), stop=(kq == 3 and kc == 15)),
                                reads=[sr, r_hT], writes=[PSR[4 + dc]], inc=(kc == 15))
                    if nfi is not None and kq == 1:
                        fr_A(nfi[ds])
                    if nfi is not None and kq == 2:
                        fr_B(nfi[ds])
                        fr_C(nfi[ds])
                for dc in range(4):
                    P.op("act", lambda e, dc=dc, ds=ds, setc=setc: e.activation(
                        out=fT2[:, dc, :], in_=PSB[4 + dc], func=AF.Identity, bias=0.0,
                        scale=modT[:, 80 + ds * 4 + dc, setc:setc + 1]),
                        reads=[PSR[4 + dc], res("modTc")], swrites=[r_fT2])
                for t in range(4):
                    bank = t
                    for dc in range(4):
                        P.op("pe", lambda e, dc=dc, t=t, bank=bank: e.transpose(
                            out=PSB[bank][:, dc * 128:(dc + 1) * 128], in_=fT2[:, dc, t * 128:(t + 1) * 128],
                            identity=ident_f), reads=[r_fT2, res("ident_f")], writes=[PSR[bank]], inc=(dc == 3))
                    P.op("dve", lambda e, t=t, ds=ds, bank=bank: e.scalar_tensor_tensor(
                        out=xres2[t][:, ds * 512:(ds + 1) * 512], in0=xres2[t][:, ds * 512:(ds + 1) * 512],
                        scalar=ALPHA, in1=PSB[bank], op0=ALU.mult, op1=ALU.add),
                        reads=[PSR[bank], xres2_r[t]], writes=[xres2_r[t]])
        for t in range(4):
            ln2_stage(4, t)

        P.op("pe", lambda e: e.transpose(out=PSB[7][:, 0:8], in_=ident_f[0:8, :], identity=ident_f[0:8, 0:8]),
             reads=[], writes=[PSR[7]], inc=True)
        P.barrier()

        _DBG['engs'] = engs
        with nc.Block() as block:
            @block.tensor
            def _(e):
                emit(e, engs["pe"])

            @block.scalar
            def _(e):
                emit(e, engs["act"])

            @block.vector
            def _(e):
                emit(e, engs["dve"])

            @block.gpsimd
            def _(e):
                emit(e, engs["pool"])

            @block.sync
            def _(e):
                emit(e, engs["sync"])
    return nc


_DBG = {}


def _rope_tables(hf):
    inv = (10000.0 ** (-(np.arange(32, dtype=np.float32) / 32.0))).astype(np.float32)
    out = np.zeros((32, 128, 256), np.float32)
    for lt in range(32):
        gt = (lt + hf * 16) % 32
        tok = gt * 128 + np.arange(128)
        row = (tok // 64).astype(np.float32)
        col = (tok % 64).astype(np.float32)
        ar = (row[:, None] * inv[None, :]).astype(np.float32)
        ac = (col[:, None] * inv[None, :]).astype(np.float32)
        cr, sr, cc, sc = np.cos(ar), np.sin(ar), np.cos(ac), np.sin(ac)
        out[lt, :, 0:128] = np.concatenate([cr, cr, cc, cc], axis=1)
        out[lt, :, 128:256] = np.concatenate([-sr, sr, -sc, sc], axis=1)
    return out.reshape(32 * 128, 256)


def _band(w, kind):
    half = w // 2
    A = np.zeros((128, 128), np.float64)
    if kind == "zero":
        return A.T.astype(np.float32)
    for d in range(128):
        if kind in ("mid", "first", "last"):
            lo, hi = d - half, d + half
            if kind == "first":
                cnt = hi - max(lo, 0)
            elif kind == "last":
                cnt = min(hi, 128) - lo
            else:
                cnt = w
            for s in range(max(lo, 0), min(hi, 128)):
                A[d, s] += 1.0 / cnt
            A[d, d] -= 1.0
        elif kind == "m1":
            for s in range(128):
                if s - 128 >= d - half:
                    A[d, s] += 1.0 / w
        elif kind == "p1":
            for s in range(128):
                if 128 + s <= d + half - 1:
                    A[d, s] += 1.0 / w
    return A.T.astype(np.float32)


def _bands(hf):
    kinds = [
        "mid", "m1", "p1",
        "first" if hf == 0 else "mid",
        "zero" if hf == 0 else "m1",
        "mid" if hf == 0 else "last",
        "p1" if hf == 0 else "zero",
        "first", "last",
    ]
    mats = np.zeros((128, 36, 128), np.float32)
    for ki, k in enumerate(kinds):
        for wi, w in enumerate((2, 4, 8, 16)):
            mats[:, ki * 4 + wi, :] = _band(w, k)
    return mats.reshape(128, 36 * 128)


_NC_CACHE = {}
_DBG = {}


def kernel(x_prompt, x_sample, cache_k, cache_v, c, c_ctx, w_mod, b_mod, w_in, q_gain, k_gain,
           w_pool, pool_scale, w_out, ln1_g, ln1_b, w_ff1, w_ff2, ln2_g, ln2_b):
    f = lambda a: np.ascontiguousarray(np.asarray(a, dtype=np.float32))
    x_prompt, x_sample, cache_k, cache_v, c, c_ctx = map(f, (x_prompt, x_sample, cache_k, cache_v, c, c_ctx))
    if "nc" not in _NC_CACHE:
        _NC_CACHE["nc"] = build_program()
    nc = _NC_CACHE["nc"]
    shared = {
        "w_mod": f(w_mod)[0], "b_mod": f(b_mod)[0].reshape(96, 128), "w_in": f(w_in)[0],
        "qg": f(q_gain)[0].reshape(1, 128), "kg": f(k_gain)[0].reshape(1, 128),
        "w_pool": f(w_pool)[0].reshape(1024, 256), "pool_scale": f(pool_scale)[0].reshape(8, 128),
        "w_out": f(w_out)[0], "ln1_g": f(ln1_g)[0].reshape(1, D), "ln1_b": f(ln1_b)[0].reshape(1, D),
        "ln2_g": f(ln2_g)[0].reshape(1, D), "ln2_b": f(ln2_b)[0].reshape(1, D),
        "w_ff1": f(w_ff1)[0], "w_ff2": f(w_ff2)[0], "ident": np.eye(128, dtype=np.float32),
    }
    in_maps = []
    for i in range(N_CORES):
        b, hf = i // 2, i % 2
        xb = x_sample[b]
        xs = np.ascontiguousarray(np.concatenate([xb[hf * 2048:(hf + 1) * 2048], xb[(1 - hf) * 2048:(2 - hf) * 2048]], axis=0))
        m = dict(shared)
        m["xs"] = xs
        m["xp"] = np.ascontiguousarray(x_prompt[2 * i:2 * i + 2].reshape(512, D))
        m["ck"] = np.ascontiguousarray(cache_k[b, 0].reshape(512, 256))
        m["cv"] = np.ascontiguousarray(cache_v[b, 0].reshape(512, 256))
        m["cvec"] = np.ascontiguousarray(np.stack([c[b], c_ctx], axis=0).reshape(32, 128))
        m["rope"] = _rope_tables(hf)
        m["bands"] = _bands(hf)
        in_maps.append(m)
    res = run_bass_kernel_spmd(nc, in_maps, core_ids=list(range(N_CORES)))
    rs = res.results
    if DEBUG:
        _DBG["rs"] = rs
    y_sample = np.zeros((4, 4096, D), np.float32)
    y_prompt = np.zeros((16, 256, D), np.float32)
    ctx_k = np.zeros((16, 1, 256, 2, 128), np.float32)
    ctx_v = np.zeros((16, 1, 256, 2, 128), np.float32)
    for i in range(N_CORES):
        b, hf = i // 2, i % 2
        y_sample[b, hf * 2048:(hf + 1) * 2048] = rs[i]["ys"]
        y_prompt[2 * i:2 * i + 2] = rs[i]["yp"].reshape(2, 256, D)
        ctx_k[2 * i:2 * i + 2, 0] = rs[i]["cko"].reshape(2, 256, 2, 128)
        ctx_v[2 * i:2 * i + 2, 0] = rs[i]["cvo"].reshape(2, 256, 2, 128)
    return (y_prompt, y_sample, ctx_k, ctx_v)
```
